# Optimizing a Trainium2 kernel written in Bass

```python
import math
import jax, jax.numpy as jnp
from jax import lax
import numpy as np

D_MODEL = 1024
BATCH = 16
SEQ = 256
DEPTH = 2
DEC_BATCH = 2
DEC_SEQ = 4096
PAST_LEN = 256

GRID_W = 64
Q_BLOCK = 128
D_MIX = D_MODEL
DIFF_HEADS = 4
DIFF_HEAD_DIM = 32
DIFF_V_DIM = 2 * DIFF_HEAD_DIM
DIFF_W = DIFF_HEADS * DIFF_V_DIM
S5_CH = 16
S5_W = D_MIX // 4
S5_GROUPS = S5_W // S5_CH
S5_STATE = 64
MLA_HEADS = 8
MLA_NOPE = 64
MLA_ROPE = 32
MLA_V = 64
MLA_Q_RANK = D_MODEL // 4
MLA_KV_RANK = D_MODEL // 8
MLA_W = MLA_HEADS * MLA_V
FFN_HIDDEN = ((8 * D_MODEL + 3 * 256 - 1) // (3 * 256)) * 256
IN_SIZES = (DIFF_HEADS * 2 * DIFF_HEAD_DIM, DIFF_HEADS * 2 * DIFF_HEAD_DIM, DIFF_W,
            S5_W, MLA_Q_RANK, MLA_KV_RANK, MLA_ROPE)
IN_W = sum(IN_SIZES)
IN_OFFSETS = tuple(sum(IN_SIZES[:i + 1]) for i in range(len(IN_SIZES) - 1))
ALPHA = (2 * DEPTH) ** 0.25
BETA = (8 * DEPTH) ** -0.25
LN_EPS = 1e-5
RMS_EPS = 1e-6
ROPE_BASE = 10000.0

kernel_name = "hybrid_diff_s5_mla_prefix_dit_step"

F32 = jnp.float32


def layer_norm(x, g, b):
    xf = x.astype(F32)
    mu = jnp.mean(xf, -1, keepdims=True)
    var = jnp.mean(jnp.square(xf - mu), -1, keepdims=True)
    return ((xf - mu) * lax.rsqrt(var + LN_EPS) * g.astype(F32) + b.astype(F32)).astype(x.dtype)


def rms_norm(x, g):
    xf = x.astype(F32)
    y = xf * lax.rsqrt(jnp.mean(xf * xf, -1, keepdims=True) + RMS_EPS) * g.astype(F32)
    return y.astype(x.dtype)


def grid_rope_tables(length, dim):
    rows = length // GRID_W
    row = jnp.repeat(jnp.arange(rows, dtype=F32), GRID_W)
    col = jnp.tile(jnp.arange(GRID_W, dtype=F32), rows)
    n_freq = dim // 4
    inv = ROPE_BASE ** (-jnp.arange(n_freq, dtype=F32) / n_freq)
    ang = jnp.concatenate([row[:, None] * inv, col[:, None] * inv], -1)
    return jnp.cos(ang), jnp.sin(ang)


def apply_rope(x, cos, sin):
    half = x.shape[-1] // 2
    xf = x.astype(F32)
    x1, x2 = xf[..., :half], xf[..., half:]
    shp = (1, cos.shape[0]) + (1,) * (x.ndim - 3) + (half,)
    cs, sn = cos.reshape(shp), sin.reshape(shp)
    return jnp.concatenate([x1 * cs - x2 * sn, x2 * cs + x1 * sn], -1).astype(x.dtype)


def over_query_blocks(fn, *qs):
    b, length = qs[0].shape[:2]
    nb = length // Q_BLOCK
    blocks = tuple(jnp.swapaxes(q.reshape((b, nb, Q_BLOCK) + q.shape[2:]), 0, 1) for q in qs)
    out = lax.map(lambda blk: fn(*blk), blocks)
    out = jnp.swapaxes(out, 0, 1)
    return out.reshape((b, length) + out.shape[3:])


def diff_attention(q, k, v, lam, norm_g, lam_init):
    scale = DIFF_HEAD_DIM ** -0.5

    def block(qb):
        s = jnp.einsum('bqhcd,bkhcd->bhcqk', qb, k).astype(F32) * scale
        p = jax.nn.softmax(s, axis=-1)
        a = p[:, :, 0] - lam * p[:, :, 1]
        return jnp.einsum('bhqk,bkhd->bqhd', a.astype(v.dtype), v)

    o = over_query_blocks(block, q)
    o = rms_norm(o, norm_g) * (1.0 - lam_init)
    return o.reshape(o.shape[0], o.shape[1], DIFF_W)


def mla_attention(q_nope, q_rope, k_nope, k_rope, v):
    scale = (MLA_NOPE + MLA_ROPE) ** -0.5

    def block(qn, qr):
        s = (jnp.einsum('bqhd,bkhd->bhqk', qn, k_nope)
             + jnp.einsum('bqhd,bkd->bhqk', qr, k_rope)).astype(F32) * scale
        p = jax.nn.softmax(s, axis=-1)
        return jnp.einsum('bhqk,bkhd->bqhd', p.astype(v.dtype), v)

    o = over_query_blocks(block, q_nope, q_rope)
    return o.reshape(o.shape[0], o.shape[1], MLA_W)


def s5_discretize(lam_re, lam_im, log_dt, b_re, b_im):
    lam_re, lam_im = lam_re.astype(F32), lam_im.astype(F32)
    b_re, b_im = b_re.astype(F32), b_im.astype(F32)
    dt = jnp.exp(log_dt.astype(F32))[:, None]
    mag = jnp.exp(lam_re * dt)
    ang = lam_im * dt
    a_re, a_im = mag * jnp.cos(ang), mag * jnp.sin(ang)
    den = lam_re * lam_re + lam_im * lam_im
    n_re, n_im = a_re - 1.0, a_im
    f_re = ((n_re * lam_re + n_im * lam_im) / den)[..., None]
    f_im = ((n_im * lam_re - n_re * lam_im) / den)[..., None]
    return a_re, a_im, f_re * b_re - f_im * b_im, f_re * b_im + f_im * b_re


def _complex_affine_combine(e1, e2):
    a1r, a1i, b1r, b1i = e1
    a2r, a2i, b2r, b2i = e2
    return (a2r * a1r - a2i * a1i, a2r * a1i + a2i * a1r,
            a2r * b1r - a2i * b1i + b2r, a2r * b1i + a2i * b1r + b2i)


def diag_scan(a_re, a_im, bu_re, bu_im, h0):
    if h0 is not None:
        h0_re, h0_im = h0
        bu_re = bu_re.at[:, 0].add(a_re * h0_re - a_im * h0_im)
        bu_im = bu_im.at[:, 0].add(a_re * h0_im + a_im * h0_re)
    ar = jnp.broadcast_to(a_re, bu_re.shape)
    ai = jnp.broadcast_to(a_im, bu_im.shape)
    _, _, h_re, h_im = lax.associative_scan(_complex_affine_combine, (ar, ai, bu_re, bu_im), axis=1)
    return h_re, h_im


def s5_mixer(u, lam_re, lam_im, log_dt, b_re, b_im, c_re, c_im, d_skip, w_glu, h0):
    bsz, length = u.shape[:2]
    uf = u.astype(F32).reshape(bsz, length, S5_GROUPS, S5_CH)
    y = uf * d_skip.astype(F32)
    finals = []
    for d in range(2):
        ar, ai, bbr, bbi = s5_discretize(lam_re[d], lam_im[d], log_dt[d], b_re[d], b_im[d])
        bu_re = jnp.einsum('blgh,gph->blgp', uf, bbr)
        bu_im = jnp.einsum('blgh,gph->blgp', uf, bbi)
        if d == 1:
            bu_re, bu_im = bu_re[:, ::-1], bu_im[:, ::-1]
        init = None if h0 is None else (h0[:, d, :, :, 0].astype(F32), h0[:, d, :, :, 1].astype(F32))
        h_re, h_im = diag_scan(ar, ai, bu_re, bu_im, init)
        if h0 is None:
            finals.append(jnp.stack([h_re[:, -1], h_im[:, -1]], -1))
        if d == 1:
            h_re, h_im = h_re[:, ::-1], h_im[:, ::-1]
        y = y + (jnp.einsum('gnp,blgp->blgn', c_re[d].astype(F32), h_re)
                 - jnp.einsum('gnp,blgp->blgn', c_im[d].astype(F32), h_im))
    y = jax.nn.gelu(y.reshape(bsz, length, S5_W))
    y = y * jax.nn.sigmoid(y @ w_glu.astype(F32))
    final = jnp.stack(finals, 1).astype(u.dtype) if h0 is None else None
    return y.astype(u.dtype), final


def token_mixer(h, lw, lam, lam_init, ctx):
    bsz, length, _ = h.shape
    z = h @ lw['w_in']
    dq, dk, dv, u, q_lat, kv_lat, k_rope = jnp.split(z, list(IN_OFFSETS), axis=-1)
    dq = dq.reshape(bsz, length, DIFF_HEADS, 2, DIFF_HEAD_DIM)
    dk = dk.reshape(bsz, length, DIFF_HEADS, 2, DIFF_HEAD_DIM)
    dv = dv.reshape(bsz, length, DIFF_HEADS, DIFF_V_DIM)
    ckv = rms_norm(kv_lat, lw['mla_kv_norm_g'])
    q = (rms_norm(q_lat, lw['mla_q_norm_g']) @ lw['mla_w_uq']).reshape(
        bsz, length, MLA_HEADS, MLA_NOPE + MLA_ROPE)
    q_nope, q_rope = q[..., :MLA_NOPE], q[..., MLA_NOPE:]
    if ctx is None:
        dk_all, dv_all, ckv_all, kr_all, h0 = dk, dv, ckv, k_rope, None
    else:
        c_k, c_v, c_ckv, c_kr, h0 = ctx
        cos_d, sin_d = grid_rope_tables(length, DIFF_HEAD_DIM)
        dq = apply_rope(dq, cos_d, sin_d)
        dk = apply_rope(dk, cos_d, sin_d)
        cos_m, sin_m = grid_rope_tables(length, MLA_ROPE)
        q_rope = apply_rope(q_rope, cos_m, sin_m)
        k_rope = apply_rope(k_rope, cos_m, sin_m)
        ctx_len = c_k.shape[1]
        dk_all = jnp.concatenate([dk, c_k.reshape(bsz, ctx_len, DIFF_HEADS, 2, DIFF_HEAD_DIM)], 1)
        dv_all = jnp.concatenate([dv, c_v], 1)
        ckv_all = jnp.concatenate([ckv, c_ckv], 1)
        kr_all = jnp.concatenate([k_rope, c_kr], 1)
    n_keys = ckv_all.shape[1]
    k_nope = (ckv_all @ lw['mla_w_uk']).reshape(bsz, n_keys, MLA_HEADS, MLA_NOPE)
    v_mla = (ckv_all @ lw['mla_w_uv']).reshape(bsz, n_keys, MLA_HEADS, MLA_V)

    diff_out = diff_attention(dq, dk_all, dv_all, lam, lw['diff_norm_g'], lam_init)
    s5_out, s5_final = s5_mixer(u, lw['s5_lam_re'], lw['s5_lam_im'], lw['s5_log_dt'],
                                lw['s5_b_re'], lw['s5_b_im'], lw['s5_c_re'], lw['s5_c_im'],
                                lw['s5_d'], lw['s5_w_glu'], h0)
    mla_out = mla_attention(q_nope, q_rope, k_nope, kr_all, v_mla)
    out = jnp.concatenate([diff_out, s5_out, mla_out], -1) @ lw['w_out']
    if ctx is None:
        new_ctx = (dk.reshape(bsz, length, DIFF_HEADS, 2 * DIFF_HEAD_DIM), dv, ckv, k_rope, s5_final)
        return out, new_ctx
    return out, None


def trunk_layer(x, cond, lw, layer_idx, ctx):
    mods = jax.nn.silu(cond) @ lw['w_ada'] + lw['b_ada']
    sh1, sc1, g1, sh2, sc2, g2 = jnp.split(mods[:, None, :], 6, axis=-1)
    lam_init = 0.8 - 0.6 * math.exp(-0.3 * layer_idx)
    lam = (jnp.exp(jnp.sum(lw['diff_lq1'].astype(F32) * lw['diff_lk1'].astype(F32)))
           - jnp.exp(jnp.sum(lw['diff_lq2'].astype(F32) * lw['diff_lk2'].astype(F32))) + lam_init)
    h = x * (1 + sc1) + sh1
    mix, new_ctx = token_mixer(h, lw, lam, lam_init, ctx)
    x = layer_norm(ALPHA * x + g1 * mix, lw['ln1_g'], lw['ln1_b'])
    h = x * (1 + sc2) + sh2
    f = (jax.nn.silu(h @ lw['ffn_w_gate']) * (h @ lw['ffn_w_up'])) @ lw['ffn_w_down']
    x = layer_norm(ALPHA * x + g2 * f, lw['ln2_g'], lw['ln2_b'])
    return x, new_ctx


def setup_inputs(seed: int = 0) -> dict:
    key = jax.random.key(seed)
    ks = iter(jax.random.split(key, 48))

    def nrm(shape, s=1.0):
        return jax.random.normal(next(ks), shape, F32) * s

    G, P = S5_GROUPS, S5_STATE
    lam_im_base = jnp.pi * jnp.arange(P, dtype=F32)
    return {
        'x_prompt': nrm((BATCH, SEQ, D_MODEL)),
        'x_sample': nrm((DEC_BATCH, DEC_SEQ, D_MODEL)),
        'c': nrm((DEC_BATCH, D_MODEL)),
        'cache_diff_k': nrm((DEC_BATCH, DEPTH, PAST_LEN, DIFF_HEADS, 2 * DIFF_HEAD_DIM)),
        'cache_diff_v': nrm((DEC_BATCH, DEPTH, PAST_LEN, DIFF_HEADS, DIFF_V_DIM)),
        'cache_mla_ckv': nrm((DEC_BATCH, DEPTH, PAST_LEN, MLA_KV_RANK)),
        'cache_mla_krope': nrm((DEC_BATCH, DEPTH, PAST_LEN, MLA_ROPE)),
        'state_s5': nrm((DEC_BATCH, DEPTH, 2, G, P, 2), 0.1),
        'c_ctx': nrm((D_MODEL,)),
        'w_ada': nrm((DEPTH, D_MODEL, 6 * D_MODEL), 0.5 * D_MODEL ** -0.5),
        'b_ada': nrm((DEPTH, 6 * D_MODEL), 0.02),
        'w_in': nrm((DEPTH, D_MODEL, IN_W), D_MODEL ** -0.5),
        'w_out': nrm((DEPTH, D_MIX, D_MODEL), BETA * D_MIX ** -0.5),
        'diff_lq1': nrm((DEPTH, DIFF_HEAD_DIM), 0.1),
        'diff_lk1': nrm((DEPTH, DIFF_HEAD_DIM), 0.1),
        'diff_lq2': nrm((DEPTH, DIFF_HEAD_DIM), 0.1),
        'diff_lk2': nrm((DEPTH, DIFF_HEAD_DIM), 0.1),
        'diff_norm_g': 1.0 + nrm((DEPTH, DIFF_V_DIM), 0.02),
        's5_lam_re': -0.5 + nrm((DEPTH, 2, G, P), 0.01),
        's5_lam_im': lam_im_base + nrm((DEPTH, 2, G, P), 0.01),
        's5_log_dt': jax.random.uniform(next(ks), (DEPTH, 2, G), F32, math.log(1e-3), math.log(1e-1)),
        's5_b_re': nrm((DEPTH, 2, G, P, S5_CH), (2 * S5_CH) ** -0.5),
        's5_b_im': nrm((DEPTH, 2, G, P, S5_CH), (2 * S5_CH) ** -0.5),
        's5_c_re': nrm((DEPTH, 2, G, S5_CH, P), S5_STATE ** -0.5),
        's5_c_im': nrm((DEPTH, 2, G, S5_CH, P), S5_STATE ** -0.5),
        's5_d': nrm((DEPTH, G, S5_CH)),
        's5_w_glu': nrm((DEPTH, S5_W, S5_W), S5_W ** -0.5),
        'mla_q_norm_g': 1.0 + nrm((DEPTH, MLA_Q_RANK), 0.02),
        'mla_w_uq': nrm((DEPTH, MLA_Q_RANK, MLA_HEADS * (MLA_NOPE + MLA_ROPE)), MLA_Q_RANK ** -0.5),
        'mla_kv_norm_g': 1.0 + nrm((DEPTH, MLA_KV_RANK), 0.02),
        'mla_w_uk': nrm((DEPTH, MLA_KV_RANK, MLA_HEADS * MLA_NOPE), MLA_KV_RANK ** -0.5),
        'mla_w_uv': nrm((DEPTH, MLA_KV_RANK, MLA_HEADS * MLA_V), MLA_KV_RANK ** -0.5),
        'ln1_g': 1.0 + nrm((DEPTH, D_MODEL), 0.02),
        'ln1_b': nrm((DEPTH, D_MODEL), 0.02),
        'ln2_g': 1.0 + nrm((DEPTH, D_MODEL), 0.02),
        'ln2_b': nrm((DEPTH, D_MODEL), 0.02),
        'ffn_w_gate': nrm((DEPTH, D_MODEL, FFN_HIDDEN), D_MODEL ** -0.5),
        'ffn_w_up': nrm((DEPTH, D_MODEL, FFN_HIDDEN), D_MODEL ** -0.5),
        'ffn_w_down': nrm((DEPTH, FFN_HIDDEN, D_MODEL), BETA * FFN_HIDDEN ** -0.5),
    }


def reference(x_prompt, x_sample, c, cache_diff_k, cache_diff_v, cache_mla_ckv, cache_mla_krope,
              state_s5, c_ctx, w_ada, b_ada, w_in, w_out, diff_lq1, diff_lk1, diff_lq2, diff_lk2,
              diff_norm_g, s5_lam_re, s5_lam_im, s5_log_dt, s5_b_re, s5_b_im, s5_c_re, s5_c_im,
              s5_d, s5_w_glu, mla_q_norm_g, mla_w_uq, mla_kv_norm_g, mla_w_uk, mla_w_uv,
              ln1_g, ln1_b, ln2_g, ln2_b, ffn_w_gate, ffn_w_up, ffn_w_down):
    stacked = {
        'w_ada': w_ada, 'b_ada': b_ada, 'w_in': w_in, 'w_out': w_out,
        'diff_lq1': diff_lq1, 'diff_lk1': diff_lk1, 'diff_lq2': diff_lq2, 'diff_lk2': diff_lk2,
        'diff_norm_g': diff_norm_g,
        's5_lam_re': s5_lam_re, 's5_lam_im': s5_lam_im, 's5_log_dt': s5_log_dt,
        's5_b_re': s5_b_re, 's5_b_im': s5_b_im, 's5_c_re': s5_c_re, 's5_c_im': s5_c_im,
        's5_d': s5_d, 's5_w_glu': s5_w_glu,
        'mla_q_norm_g': mla_q_norm_g, 'mla_w_uq': mla_w_uq, 'mla_kv_norm_g': mla_kv_norm_g,
        'mla_w_uk': mla_w_uk, 'mla_w_uv': mla_w_uv,
        'ln1_g': ln1_g, 'ln1_b': ln1_b, 'ln2_g': ln2_g, 'ln2_b': ln2_b,
        'ffn_w_gate': ffn_w_gate, 'ffn_w_up': ffn_w_up, 'ffn_w_down': ffn_w_down,
    }
    cond_ctx = jnp.broadcast_to(c_ctx, (x_prompt.shape[0], D_MODEL))
    y_prompt = x_prompt
    ks, vs, ckvs, krs, sts = [], [], [], [], []
    for l in range(DEPTH):
        lw = {name: arr[l] for name, arr in stacked.items()}
        y_prompt, (k_l, v_l, ckv_l, kr_l, st_l) = trunk_layer(y_prompt, cond_ctx, lw, l, None)
        ks.append(k_l); vs.append(v_l); ckvs.append(ckv_l); krs.append(kr_l); sts.append(st_l)
    y_sample = x_sample
    for l in range(DEPTH):
        lw = {name: arr[l] for name, arr in stacked.items()}
        ctx = (cache_diff_k[:, l], cache_diff_v[:, l], cache_mla_ckv[:, l], cache_mla_krope[:, l],
               state_s5[:, l])
        y_sample, _ = trunk_layer(y_sample, c, lw, l, ctx)
    new_diff_k = jnp.stack(ks, 1)
    new_diff_v = jnp.stack(vs, 1)
    new_mla_ckv = jnp.stack(ckvs, 1)
    new_mla_krope = jnp.stack(krs, 1)
    new_s5_state = jnp.stack(sts, 1)
    return (y_prompt, y_sample, new_diff_k, new_diff_v, new_mla_ckv, new_mla_krope, new_s5_state)
```

```python
import math
import numpy as np
from concourse.bass_utils import run_bass_kernel_spmd
import contextlib
import concourse.bass as bass
import concourse.mybir as mybir

F32 = mybir.dt.float32
BF16 = mybir.dt.bfloat16
I32 = mybir.dt.int32
AF = mybir.ActivationFunctionType
ALU = mybir.AluOpType

PE, ACT, DVE, POOL, SP = "pe", "act", "dve", "pool", "sp"
ENGINES = (PE, ACT, DVE, POOL, SP)
N_DMA_SEMS = 48


def region_of(ap):
    t = ap.tensor
    name = t.name
    dims = [(int(s), int(c)) for s, c in ap.ap]
    off = int(ap.offset)
    space = str(ap.space) if hasattr(ap, "space") else ""
    shape = [int(s) for s in t.shape]
    if "DRAM" in space.upper() or "dram" in type(t).__name__.lower() or "DRam" in type(t).__name__:
        lo = off + sum(min(0, s * (c - 1)) for s, c in dims)
        hi = off + sum(max(0, s * (c - 1)) for s, c in dims) + 1
        return (name, 0, 1, lo, hi)
    if "PSum" in type(t).__name__:
        return ("%PSUM%" + name, 0, 128, 0, 1 << 30)
    fsz = 1
    for s in shape[1:]:
        fsz *= s
    p0 = off // fsz
    f0 = off % fsz
    if dims and dims[0][0] == fsz:
        npart = dims[0][1]
        rest = dims[1:]
    elif dims and dims[0][0] == 0 and len(dims) > 1:
        npart = 1
        rest = dims[1:]
    else:
        npart = 1
        rest = dims
    lo = f0 + sum(min(0, s * (c - 1)) for s, c in rest)
    hi = f0 + sum(max(0, s * (c - 1)) for s, c in rest) + 1
    return (name, p0, p0 + npart, lo, hi)


def overlap(a, b):
    return a[1] < b[2] and b[1] < a[2] and a[3] < b[4] and b[3] < a[4]


class Op:
    __slots__ = ("eng", "fn", "idx", "waits", "signal", "dma_sem", "dma_val", "is_dma", "name", "sigval")

    def __init__(self, eng, fn, name=""):
        self.eng = eng
        self.fn = fn
        self.idx = None
        self.waits = {}
        self.signal = False
        self.is_dma = False
        self.dma_sem = None
        self.dma_val = None
        self.name = name


class Prog:
    def __init__(self, nc):
        self.nc = nc
        self.ops = {e: [] for e in ENGINES}
        self.recs = {}
        self.dma_count = 0
        self.dma_count_sw = 0
        self.dma_last = [None] * N_DMA_SEMS
        self.stack = contextlib.ExitStack()
        self.n_ops = 0

    def sbuf(self, name, shape, dtype):
        return self.stack.enter_context(self.nc.sbuf_tensor(name, list(shape), dtype))

    def psum(self, name, shape, dtype=F32):
        return self.stack.enter_context(self.nc.psum_tensor(name, list(shape), dtype))

    def _deps(self, op, reads, writes, pe_accum=False):
        deps = []
        rr = [region_of(a) for a in reads]
        ww = [region_of(a) for a in writes]
        for reg, kind in [(r, "r") for r in rr] + [(w, "w") for w in ww]:
            lst = self.recs.setdefault(reg[0], [])
            for rec in lst:
                oreg, okind, oop = rec
                if okind == "r" and kind == "r":
                    if not (reg[0].startswith("%PSUM%") and oop.eng != op.eng):
                        continue
                if not overlap(reg, oreg):
                    continue
                if oop is op:
                    continue
                if oop.eng == op.eng and not oop.is_dma and not op.is_dma:
                    if op.eng == PE:
                        continue
                deps.append(oop)
        for reg, kind in [(r, "r") for r in rr] + [(w, "w") for w in ww]:
            lst = self.recs.setdefault(reg[0], [])
            new = []
            for rec in lst:
                oreg, okind, oop = rec
                if kind == "w" and (not reg[0].startswith("%PSUM%")) and oreg[1] >= reg[1] and oreg[2] <= reg[2] and oreg[3] >= reg[3] and oreg[4] <= reg[4]:
                    continue
                if reg[0].startswith("%PSUM%") and oop.eng == op.eng and not op.is_dma:
                    continue
                if oop.eng == op.eng and (not oop.is_dma) and (not op.is_dma) and okind == kind and oreg == reg:
                    continue
                new.append(rec)
            new.append((reg, kind, op))
            self.recs[reg[0]] = new
        return deps

    def _add_waits(self, op, deps):
        for d in deps:
            if d.is_dma:
                key = ("dma", d.dma_sem)
                val = d.dma_val
            else:
                d.signal = True
                key = d.eng
                val = d
            cur = op.waits.get(key)
            if cur is None:
                op.waits[key] = val
            else:
                if d.is_dma:
                    op.waits[key] = max(cur, val)
                else:
                    op.waits[key] = cur if cur.idx >= d.idx else d

    def add(self, eng, fn, reads=(), writes=(), name=""):
        op = Op(eng, fn, name)
        op.idx = len(self.ops[eng])
        deps = self._deps(op, reads, writes)
        self._add_waits(op, deps)
        self.ops[eng].append(op)
        self.n_ops += 1
        return op

    def _dma_slot(self, op, eng):
        half = N_DMA_SEMS // 2
        if eng == POOL:
            c = self.dma_count_sw
            self.dma_count_sw += 1
            k = half + c % half
        else:
            c = self.dma_count
            self.dma_count += 1
            k = c % half
        op.dma_sem = k
        op.dma_val = 16 * (c // half + 1)
        return k

    def dma(self, out, in_, eng=SP, extra_reads=(), extra_writes=(), **kw):
        def fn(e, out=out, in_=in_, kw=kw):
            return e.dma_start(out=out, in_=in_, **kw)
        op = Op(eng, fn, "dma")
        op.is_dma = True
        op.idx = len(self.ops[eng])
        k = self._dma_slot(op, eng)
        prev = self.dma_last[k]
        deps = self._deps(op, [in_] + list(extra_reads), [out] + list(extra_writes))
        if prev is not None:
            deps.append(prev)
        self.dma_last[k] = op
        self._add_waits(op, deps)
        self.ops[eng].append(op)
        self.n_ops += 1
        return op

    def dma_custom(self, op, reads, writes):
        op.is_dma = True
        eng = op.eng
        op.idx = len(self.ops[eng])
        k = self._dma_slot(op, eng)
        prev = self.dma_last[k]
        deps = self._deps(op, list(reads), list(writes))
        if prev is not None:
            deps.append(prev)
        self.dma_last[k] = op
        self._add_waits(op, deps)
        self.ops[eng].append(op)
        self.n_ops += 1
        return op

    def collective(self, fn, reads, writes):
        op = Op(POOL, fn, "cc")
        op.is_dma = True
        op.idx = len(self.ops[POOL])
        if not hasattr(self, "n_cc"):
            self.n_cc = 0
        op.dma_sem = "cc%d" % self.n_cc
        self.n_cc += 1
        op.dma_val = 1
        deps = self._deps(op, reads, writes)
        self._add_waits(op, deps)
        self.ops[POOL].append(op)
        self.n_ops += 1
        return op

    def mm(self, out, lhsT, rhs, start=True, stop=True, **kw):
        def fn(e):
            return e.matmul(out, lhsT, rhs, start=start, stop=stop, **kw)
        return self.add(PE, fn, reads=[lhsT, rhs], writes=[out], name="mm")

    def transpose(self, out, in_, ident):
        def fn(e):
            return e.transpose(out, in_, ident)
        return self.add(PE, fn, reads=[in_, ident], writes=[out], name="tr")

    def act(self, out, in_, func, bias=None, scale=None, accum_out=None, eng=ACT):
        kw = {}
        reads = [in_]
        if bias is not None:
            kw["bias"] = bias
            if not isinstance(bias, (int, float)):
                reads.append(bias)
        if scale is not None:
            kw["scale"] = scale
            if not isinstance(scale, (int, float)):
                reads.append(scale)
        writes = [out]
        if accum_out is not None:
            kw["accum_out"] = accum_out
            writes.append(accum_out)

        def fn(e):
            return e.activation(out, in_, func, **kw)
        return self.add(eng, fn, reads=reads, writes=writes, name="act")

    def tt(self, out, in0, in1, op, eng=DVE):
        def fn(e):
            return e.tensor_tensor(out, in0, in1, op)
        return self.add(eng, fn, reads=[in0, in1], writes=[out], name="tt")

    def ts(self, out, in0, s1, s2, op0, op1=None, eng=DVE, accum_out=None):
        reads = [in0]
        if s1 is not None and not isinstance(s1, (int, float)):
            reads.append(s1)
        if s2 is not None and not isinstance(s2, (int, float)):
            reads.append(s2)
        writes = [out]
        kw = {}
        if accum_out is not None:
            kw["accum_out"] = accum_out
            writes.append(accum_out)

        def fn(e):
            if op1 is None:
                return e.tensor_scalar(out, in0, s1, None, op0, **kw)
            return e.tensor_scalar(out, in0, s1, s2, op0, op1, **kw)
        return self.add(eng, fn, reads=reads, writes=writes, name="ts")

    def stt(self, out, in0, scalar, in1, op0, op1, eng=DVE):
        reads = [in0, in1]
        if not isinstance(scalar, (int, float)):
            reads.append(scalar)

        def fn(e):
            return e.scalar_tensor_tensor(out, in0, scalar, in1, op0, op1)
        return self.add(eng, fn, reads=reads, writes=[out], name="stt")

    def scan(self, out, d0, d1, initial, op0=ALU.mult, op1=ALU.add):
        reads = [d0, d1]
        if not isinstance(initial, (int, float)):
            reads.append(initial)

        def fn(e):
            return e.tensor_tensor_scan(out, d0, d1, initial, op0, op1)
        return self.add(DVE, fn, reads=reads, writes=[out], name="scan")

    def copy(self, out, in_, eng=DVE):
        if eng == ACT:
            def fn(e):
                return e.copy(out, in_)
        else:
            def fn(e):
                return e.tensor_copy(out, in_)
        return self.add(eng, fn, reads=[in_], writes=[out], name="copy")

    def memset(self, ap, val, eng=DVE):
        def fn(e):
            return e.memset(ap, val)
        return self.add(eng, fn, reads=[], writes=[ap], name="memset")

    def recip(self, out, in_):
        def fn(e):
            return e.reciprocal(out, in_)
        return self.add(DVE, fn, reads=[in_], writes=[out], name="recip")

    def recip_fast(self, out, in_):
        def fn(e):
            return e.reciprocal_approx_fast(out, in_)
        return self.add(DVE, fn, reads=[in_], writes=[out], name="recipf")

    def emit(self, final_wait_all=True):
        nc = self.nc
        for e in ENGINES:
            cnt = 0
            for op in self.ops[e]:
                if op.is_dma:
                    continue
                if op.signal:
                    cnt += 1
                    op.sigval = cnt
        sems = {}
        st = self.stack
        for e in ENGINES:
            sems[e] = st.enter_context(nc.semaphore("sem_" + e))
        dsems = [st.enter_context(nc.semaphore("dsem%d" % i)) for i in range(N_DMA_SEMS)]
        ccsems = {"cc%d" % i: st.enter_context(nc.semaphore("ccsem%d" % i)) for i in range(getattr(self, "n_cc", 0))}
        engobj = {PE: "tensor", ACT: "scalar", DVE: "vector", POOL: "gpsimd", SP: "sync"}
        prog = self

        def run_engine(ename, eng):
            waited = {}
            for op in prog.ops[ename]:
                for key, val in op.waits.items():
                    if isinstance(key, tuple):
                        sem = ccsems[key[1]] if isinstance(key[1], str) else dsems[key[1]]
                        v = val
                    else:
                        sem = sems[key]
                        v = val.sigval
                    if waited.get(key, 0) >= v:
                        continue
                    waited[key] = v
                    eng.wait_ge(sem, v)
                ins = op.fn(eng)
                if op.is_dma and isinstance(op.dma_sem, str):
                    ins.then_inc(ccsems[op.dma_sem], 1)
                elif op.is_dma:
                    ins.then_inc(dsems[op.dma_sem], 16)
                elif op.signal:
                    ins.then_inc(sems[ename], 1)
            if ename == SP and final_wait_all:
                for k in range(N_DMA_SEMS):
                    last = prog.dma_last[k]
                    if last is not None and waited.get(("dma", k), 0) < last.dma_val:
                        eng.wait_ge(dsems[k], last.dma_val)

        with nc.Block() as block:
            @block.tensor
            def _(e):
                run_engine(PE, e)

            @block.scalar
            def _(e):
                run_engine(ACT, e)

            @block.vector
            def _(e):
                run_engine(DVE, e)

            @block.gpsimd
            def _(e):
                run_engine(POOL, e)

            @block.sync
            def _(e):
                run_engine(SP, e)
        self.stack.close()


from concourse.bass import ds

NL = 2
D = 1024
T = 1536
NPR = 512
NS = 1024
NK = 4352
NKT = 34
ALPHA = (2 * NL) ** 0.25
LN_EPS = 1e-5
RMS_EPS = 1e-6
FFN_H = 2816
NJ = 22
AX = mybir.AxisListType
GRP = [[0, 1, 2, 3], [4, 5, 6, 7]]
SH = 448
E1R = 4 * SH + 160


def lam_init(l):
    return 0.8 - 0.6 * math.exp(-0.3 * l)


class KB:
    def __init__(self, stop_after=None, dbg=False):
        self.stop_after = stop_after
        self.dbg = dbg
        self.nc = bass.Bass("TRN2", target_bir_lowering=False)
        self.P = Prog(self.nc)
        self.ins = {}
        self.outs = {}
        self._bank_rr = 0
        self._s_rr = 0
        self._pt_rr = 0
        self._stg_rr = 0
        self._cp_rr = 0
        import os
        self.post_eng = POOL if os.environ.get('POST_POOL', '0') == '1' else DVE
        self.prefetch = os.environ.get('PREFETCH', '1') == '1'

    def din(self, name, shape, dt=F32):
        ap = self.nc.dram_tensor(name, list(shape), dt, kind="ExternalInput").ap()
        self.ins[name] = ap
        return ap

    def dout(self, name, shape, dt=F32):
        ap = self.nc.dram_tensor(name, list(shape), dt, kind="ExternalOutput").ap()
        self.outs[name] = ap
        return ap

    def dint(self, name, shape, dt=BF16):
        return self.nc.dram_tensor(name, list(shape), dt, kind="Internal").ap()

    def av(self, off, shape, dtype=BF16):
        n = 1
        for s in shape[1:]:
            n *= s
        if dtype == F32:
            n *= 2
        v = self.ARA[0:shape[0], off:off + n]
        if dtype == F32:
            v = v.bitcast(F32)
        if len(shape) == 3:
            v = v.rearrange("p (a b) -> p a b", b=shape[2])
        elif len(shape) == 4:
            v = v.rearrange("p (a b c) -> p a b c", b=shape[2], c=shape[3])
        elif len(shape) == 5:
            v = v.rearrange("p (a b c d) -> p a b c d", b=shape[2], c=shape[3], d=shape[4])
        return v

    def bv(self, off, shape, dtype=BF16):
        n = 1
        for s in shape[1:]:
            n *= s
        if dtype == F32:
            n *= 2
        v = self.ARB[0:shape[0], off:off + n]
        if dtype == F32:
            v = v.bitcast(F32)
        if len(shape) == 3:
            v = v.rearrange("p (a b) -> p a b", b=shape[2])
        elif len(shape) == 4:
            v = v.rearrange("p (a b c) -> p a b c", b=shape[2], c=shape[3])
        elif len(shape) == 5:
            v = v.rearrange("p (a b c d) -> p a b c d", b=shape[2], c=shape[3], d=shape[4])
        return v

    def bank(self):
        b = self.B[(0, 1, 2, 4, 5)[self._bank_rr % 5]]
        self._bank_rr += 1
        return b

    def sbank(self):
        b = self.B[self._s_rr % 3]
        self._s_rr += 1
        return b

    def ptbuf(self):
        b = self.PT[:, self._pt_rr % 6, :]
        self._pt_rr += 1
        return b

    def stgbuf(self):
        b = self.STG[:, self._stg_rr % 4, :]
        self._stg_rr += 1
        return b

    def cpeng(self):
        self._cp_rr += 1
        return DVE if self._cp_rr % 2 else ACT

    def reduce_sum(self, out, in_):
        def fn(e):
            return e.tensor_reduce(out, in_, AX.X, ALU.add)
        return self.P.add(DVE, fn, reads=[in_], writes=[out], name="red")

    def declare(self):
        P = self.P
        d = self.din
        self.x_tok = d("x_tok", [T, D])
        self.condT = d("condT", [128, 8, 2])
        self.w_ada = d("w_ada", [NL, D, 6 * D])
        self.badaT = d("badaT", [128, NL, 48])
        self.w_in = d("w_in", [NL, D, 1440])
        self.w_out = d("w_out", [NL, D, D])
        self.lqk = d("lqk", [128, NL, 128])
        self.dng = d("dng", [64, NL])
        self.gq = d("gq", [128, NL, 2])
        self.gkv = d("gkv", [128, NL])
        self.gkv_b = d("gkv_b", [NL, 128, 128])
        self.w_uq = d("w_uq", [NL, 256, 768])
        self.w_uk = d("w_uk", [NL, 128, 512])
        self.w_uk_own = d("w_uk_own", [NL, 128, 128])
        self.w_uv = d("w_uv", [NL, 128, 512])
        self.w_uv_own = d("w_uv_own", [NL, 128, 128])
        self.lnp = d("lnp", [128, NL, 4, 8])
        self.w_gate = d("w_gate", [NL, D, FFN_H])
        self.w_up = d("w_up", [NL, D, FFN_H])
        self.w_down = d("w_down", [NL, FFN_H, D])
        self.s5p = d("s5p", [NL, 128, 3, 16])
        self.s5ps = d("s5ps", [NL, 128, 3, 4])
        self.s5b = d("s5b", [NL, 16, 2, 128, 16])
        self.s5bs = d("s5bs", [NL, 4, 2, 128, 16])
        self.s5c = d("s5c", [NL, 16, 2, 32, 64])
        self.s5cs = d("s5cs", [NL, 4, 2, 32, 64])
        self.s5d = d("s5d", [128, NL, 2])
        self.s5ds = d("s5ds", [64, NL])
        self.w_glu = d("w_glu", [NL, 256, 256])
        self.s5h0 = d("s5h0", [NL, 4, 128, 2])
        self.c_dk = d("c_dk", [NL, 256, 64])
        self.c_dv = d("c_dv", [NL, 256, 64])
        self.c_ckv = d("c_ckv", [NL, 256, 128])
        self.c_kr = d("c_kr", [NL, 256, 32])
        self.rope = d("rope", [2, 128, NS])
        self.identd = d("identd", [128, 128])
        o = self.dout
        self.y_tok = o("y_tok", [T, D])
        self.ndk = o("ndk", [2, NL, 256, 256])
        self.ndv = o("ndv", [2, NL, 256, 256])
        self.nckv = o("nckv", [2, NL, 256, 128])
        self.nkr = o("nkr", [2, NL, 256, 32])
        self.ns5 = o("ns5", [2, NL, 2, 16, 64, 2])
        self.x1_in = [self.dint("x1_in%d" % l, [4, SH, NS]) for l in range(NL)]
        self.x1_out = [self.dint("x1_out%d" % l, [4, 4 * SH, NS]) for l in range(NL)]
        self.g1_in = [self.dint("g1_in%d" % l, [160, NS]) for l in range(NL)]
        self.g1_out = [self.dint("g1_out%d" % l, [4 * 160, NS]) for l in range(NL)]
        self.own1 = [self.dint("own1_%d" % l, [4, SH, NS]) for l in range(NL)]
        self.own2 = [self.dint("own2_%d" % l, [4, 256, NS]) for l in range(NL)]
        self.e2_in = [self.dint("e2_in%d" % l, [256, 4096]) for l in range(NL)]
        self.e2_out = [self.dint("e2_out%d" % l, [2, 4 * 128, 4096]) for l in range(NL)]
        s = P.sbuf
        self.xT = s("xT", [128, 8, T], F32)
        self.ARA = s("ara", [128, 48160], BF16)
        self.ARB = s("arb", [128, 12288], BF16)
        self.B = [P.psum("B%d" % i, [128, 512], F32) for i in range(8)]
        self.ident_f = s("ident_f", [128, 128], F32)
        self.ident_b = s("ident_b", [128, 128], BF16)
        self.ones_f = s("ones_f", [128, 128], F32)
        self.ones_b = s("ones_b", [128, 128], BF16)
        self.eps_ln = s("eps_ln", [128, 1], F32)
        self.eps_rms = s("eps_rms", [128, 1], F32)
        self.MS = [s("MS%d" % l, [128, 48, 2], F32) for l in range(NL)]
        self.scb = s("scb", [128, 8, 2], BF16)
        self.cond_s = s("cond_s", [128, 8, 2], F32)
        self.bada_s = s("bada_s", [128, NL, 48], F32)
        self.lnp_s = s("lnp_s", [128, NL, 4, 8], F32)
        self.gq_s = s("gq_s", [128, NL, 2], F32)
        self.gkv_s = s("gkv_s", [128, NL], F32)
        self.dng_s = s("dng_s", [64, NL], F32)
        self.gsc = s("gsc", [64, NL], F32)
        self.lamt = s("lamt", [128, NL], F32)
        self.wuq = s("wuq", [128, 2, 768], BF16)
        self.wuk = s("wuk", [128, 512], BF16)
        self.wuko = s("wuko", [128, 128], BF16)
        self.wuv = s("wuv", [128, 512], BF16)
        self.wuvo = s("wuvo", [128, 128], BF16)
        self.wglu = s("wglu", [128, 2, 256], BF16)
        self.gkvb_s = s("gkvb_s", [128, 128], F32)
        self.F1 = s("F1", [128, 512], F32)
        self.F2 = s("F2", [128, 512], F32)
        self.F3 = s("F3", [128, 512], F32)
        self.F4 = s("F4", [128, 512], F32)
        self.STG = s("STG", [128, 4, 512], BF16)
        self.PT = s("PT", [128, 6, 512], BF16)
        self.QO = s("QO", [128, 3, 512], BF16)
        self.SQB = s("SQB", [128, 2, 512], BF16)
        self.small = s("small", [128, 64], F32)


A_HT = 0
A_ACT = 12288
A_WIN = 12288
A_WINR = 23808
A_WKR = 28160
A_WKRR = 28928
A_DQP = 29696
A_KDP = 30720
A_UTP = 31744
A_QNP = 32768
A_CKP = 33792
A_KRP = 34304
A_VDP = 34816
A_CAT = 35872
A_KDO = 0
A_VDO = 4352
A_CKB = 6596
A_KH = 12288
A_VH = 20992
A_KHP = 25480
A_VHP = 25992
A_XIN = 12288
A_WAB = 16384
A_WOUT = 0
A_YST = 8192
A_YP = 26256
A_YS = 7620
VW = 66
B_WGU = 0
B_WD = 6144
B_TAB = 0
B_CLHS = 4096
B_BLHS = 6144
B_BD = 6656
B_WORK = 7168
B_UCH = 10240
B_RR = 10752


def _stage0(self):
    P = self.P
    P.dma(self.ident_f[:], self.identd)
    P.copy(self.ident_b[:], self.ident_f[:])
    P.memset(self.ones_f[:], 1.0)
    P.memset(self.ones_b[:], 1.0)
    P.memset(self.eps_ln[:], LN_EPS / (ALPHA * ALPHA))
    P.memset(self.eps_rms[:], RMS_EPS)
    P.dma(self.cond_s[:], self.condT)
    P.dma(self.bada_s[:], self.badaT)
    P.dma(self.lnp_s[:], self.lnp)
    P.dma(self.gq_s[:], self.gq)
    P.dma(self.gkv_s[:], self.gkv)
    P.dma(self.dng_s[:], self.dng)
    lqk_s = self.F4[:, 0:256].rearrange('p (l n) -> p l n', n=128)
    P.dma(lqk_s, self.lqk)
    xin = self.av(A_XIN, [128, 2, D], F32)
    for tt in range(12):
        buf = xin[:, tt % 2, :]
        P.dma(buf, self.x_tok[tt * 128:(tt + 1) * 128, :])
        for half in range(2):
            bk = self.B[(tt * 2 + half) % 4]
            for kk in range(4):
                k = half * 4 + kk
                P.transpose(bk[:, kk * 128:(kk + 1) * 128], buf[:, k * 128:(k + 1) * 128], self.ident_f[:])
            P.copy(self.xT[:, half * 4:half * 4 + 4, tt * 128:(tt + 1) * 128],
                   bk[:].rearrange("p (k t) -> p k t", t=128), eng=self.cpeng())
    sm = self.small
    for l in range(NL):
        q = lqk_s[:, l, :]
        P.tt(sm[:, 0:32], q[:, 0:32], q[:, 32:64], ALU.mult)
        P.tt(sm[:, 32:64], q[:, 64:96], q[:, 96:128], ALU.mult)
        self.reduce_sum(self.F1[:, 0:1], sm[:, 0:32])
        self.reduce_sum(self.F1[:, 1:2], sm[:, 32:64])
        P.act(self.F1[:, 2:4], self.F1[:, 0:2], AF.Exp)
        P.tt(self.F1[:, 4:5], self.F1[:, 2:3], self.F1[:, 3:4], ALU.subtract)
        P.ts(self.lamt[:, l:l + 1], self.F1[:, 4:5], lam_init(l), None, ALU.add)
        P.ts(self.gsc[:, l:l + 1], self.dng_s[:, l:l + 1], 1.0 - lam_init(l), None, ALU.mult)
    P.act(self.scb[:], self.cond_s[:], AF.Silu)
    self.mods(0, part=0)


def _mods(self, l, part=None):
    P = self.P
    wab = self.av(A_WAB, [128, 2, 8, 256])
    c0, c1 = (0, 24) if part is None else ((0, 8) if part == 0 else (8, 24))
    for c in range(c0, c1):
        wa = wab[:, c % 2]
        P.dma(wa, self.w_ada[l, :, c * 256:(c + 1) * 256].rearrange("(k p) n -> p k n", p=128), eng=POOL)
        for j in range(2):
            jj = c * 2 + j
            for k in range(8):
                P.mm(self.B[7][:, jj * 2:jj * 2 + 2], wa[:, k, j * 128:(j + 1) * 128], self.scb[:, k, :],
                     start=(k == 0), stop=(k == 7))
    MS = self.MS[l]
    psv = self.B[7][:, 0:96].rearrange("p (j r) -> p j r", r=2)
    j0, j1 = 2 * c0, 2 * c1
    for r in range(2):
        P.tt(MS[:, j0:j1, r], psv[:, j0:j1, r], self.bada_s[:, l, j0:j1], ALU.add)
    if j0 <= 8 < j1:
        P.ts(MS[:, 8:16, :], MS[:, 8:16, :], 1.0, None, ALU.add)
    if j0 <= 32 < j1:
        P.ts(MS[:, 32:40, :], MS[:, 32:40, :], 1.0, None, ALU.add)
        P.ts(MS[:, 16:24, :], MS[:, 16:24, :], 1.0 / ALPHA, None, ALU.mult)
        P.ts(MS[:, 40:48, :], MS[:, 40:48, :], 1.0 / ALPHA, None, ALU.mult)


def _modulate(self, l, which):
    P = self.P
    hT = self.av(A_HT, [128, 8, T])
    MS = self.MS[l]
    jsh = 0 if which == 1 else 24
    jsc = 8 if which == 1 else 32
    for k in range(8):
        P.ts(hT[:, k, 0:NPR], self.xT[:, k, 0:NPR], MS[:, jsc + k, 0:1], MS[:, jsh + k, 0:1], ALU.mult, ALU.add)
        P.ts(hT[:, k, NPR:T], self.xT[:, k, NPR:T], MS[:, jsc + k, 1:2], MS[:, jsh + k, 1:2], ALU.mult, ALU.add)
    return hT


def _load_layer_weights(self, l):
    P = self.P
    win = self.av(A_WIN, [128, 8, 1440])
    winr = self.av(A_WINR, [128, 8, 544])
    wkr = self.av(A_WKR, [128, 8, 96])
    wkrr = self.av(A_WKRR, [128, 8, 96])
    self.ropet = self.av(A_CAT, [128, 2, NS])
    self.wuqr = self.av(A_CAT + 2048, [128, 2, 768])
    P.dma(self.ropet[:], self.rope.rearrange("a p n -> p a n"), eng=POOL)
    src = self.w_in[l].rearrange("(k p) n -> p k n", p=128)
    P.dma(win[:, 0:4, :], src[:, 0:4, :], eng=POOL)
    P.dma(win[:, 4:8, :], src[:, 4:8, :], eng=POOL)
    P.dma(self.wuq[:], self.w_uq[l].rearrange("(k p) n -> p k n", p=128), eng=POOL)
    P.dma(self.wuk[:], self.w_uk[l], eng=POOL)
    P.dma(self.wuv[:], self.w_uv[l], eng=POOL)
    P.dma(self.wuko[:], self.w_uk_own[l], eng=POOL)
    P.dma(self.wuvo[:], self.w_uv_own[l], eng=POOL)
    P.dma(self.wglu[:], self.w_glu[l].rearrange("(k p) n -> p k n", p=128), eng=POOL)
    P.dma(self.gkvb_s[:], self.gkv_b[l])
    for k in range(8):
        s4 = win[:, k, 0:512].rearrange("p (b h d) -> p b h d", h=2, d=16)
        d4 = winr[:, k, 0:512].rearrange("p (b h d) -> p b h d", h=2, d=16)
        P.ts(d4[:, :, 0, :], s4[:, :, 1, :], -1.0, None, ALU.mult)
        P.copy(d4[:, :, 1, :], s4[:, :, 0, :])
    P.memset(wkr[:], 0.0)
    P.memset(wkrr[:], 0.0)
    P.copy(wkr[:, :, 64:96], win[:, :, 1408:1440])
    P.ts(wkrr[:, :, 64:80], win[:, :, 1424:1440], -1.0, None, ALU.mult)
    P.copy(wkrr[:, :, 80:96], win[:, :, 1408:1424])
    for kt in range(2):
        P.ts(self.wuq[:, kt, :], self.wuq[:, kt, :], self.gq_s[:, l, kt:kt + 1], None, ALU.mult)
    P.memset(self.wuqr[:], 0.0)
    for kt in range(2):
        s3 = self.wuq[:, kt, :].rearrange("p (h c) -> p h c", c=96)
        d3 = self.wuqr[:, kt, :].rearrange("p (h c) -> p h c", c=96)
        P.ts(d3[:, :, 64:80], s3[:, :, 80:96], -1.0, None, ALU.mult)
        P.copy(d3[:, :, 80:96], s3[:, :, 64:80])
    return win, winr, wkr, wkrr


def _proj(self, lhs_fn, t0, n, nrows=128):
    P = self.P
    hT = self.av(A_HT, [128, 8, T])
    b = self.bank()
    for k in range(8):
        P.mm(b[0:nrows, 0:n], lhs_fn(k), hT[:, k, t0:t0 + n], start=(k == 0), stop=(k == 7))
    return b


def _rms_fm(self, pbanks, n, nfeat):
    P = self.P
    for i, pb in enumerate(pbanks):
        P.act(self.SQB[:, i, 0:n], pb[:, 0:n], AF.Square)
    ssb = self.B[6]
    for i in range(len(pbanks)):
        P.mm(ssb[:, 0:n], self.ones_b[:], self.SQB[:, i, 0:n], start=(i == 0), stop=(i == len(pbanks) - 1))
    P.act(self.F3[:, 0:n], ssb[:, 0:n], AF.Ln, bias=self.eps_rms[:, 0:1], scale=1.0 / nfeat)
    P.act(self.F4[:, 0:n], self.F3[:, 0:n], AF.Exp, scale=-0.5)
    return self.F4


def _phaseC(self, l, win, winr, wkr, wkrr):
    P = self.P
    hT = self.av(A_HT, [128, 8, T])
    x1 = self.x1_in[l]
    g1 = self.g1_in[l]
    cosT = self.ropet[:, 0, :]
    sinT = self.ropet[:, 1, :]
    n = NPR
    DQP = self.av(A_DQP, [128, 2, 512])
    KDP = self.av(A_KDP, [128, 2, 512])
    UTP = self.av(A_UTP, [128, 2, 512])
    QNP = self.av(A_QNP, [128, 2, 512])
    CKP = self.av(A_CKP, [128, 512])
    KRP = self.av(A_KRP, [128, 512])
    VDP = self.av(A_VDP, [128, 4, 4, VW])
    P.memset(VDP[:], 1.0)
    for j in range(2):
        b = self.proj(lambda k, j=j: win[:, k, j * 128:(j + 1) * 128], 0, n)
        P.copy(DQP[:, j, :], b[:, 0:n], eng=self.cpeng())
        b = self.proj(lambda k, j=j: win[:, k, 256 + j * 128:256 + (j + 1) * 128], 0, n)
        P.copy(KDP[:, j, :], b[:, 0:n], eng=self.cpeng())
        b = self.proj(lambda k, j=j: win[:, k, 768 + j * 128:768 + (j + 1) * 128], 0, n)
        P.copy(UTP[:, j, :], b[:, 0:n], eng=self.cpeng())
    bq = [self.proj(lambda k, j=j: win[:, k, 1024 + j * 128:1024 + (j + 1) * 128], 0, n) for j in range(2)]
    rstd = self.rms_fm(bq, n, 256)
    for j in range(2):
        P.tt(QNP[:, j, :], bq[j][:, 0:n], rstd[:, 0:n], ALU.mult)
    bk = self.proj(lambda k: win[:, k, 1280:1408], 0, n)
    rstd = self.rms_fm([bk], n, 128)
    P.stt(CKP[:, :], bk[:, 0:n], self.gkv_s[:, l:l + 1], rstd[:, 0:n], ALU.mult, ALU.mult)
    if self.stop_after == "c1":
        return
    b = self.proj(lambda k: wkr[:, k, :], 0, n, nrows=96)
    P.copy(KRP[64:96, :], b[64:96, 0:n])
    if self.stop_after == "c2":
        return
    for tt in range(4):
        seq, t0 = tt // 2, (tt % 2) * 128
        b1 = self.bank()
        b2 = self.bank()
        for k in range(8):
            P.mm(b1[:, 0:512], hT[:, k, tt * 128:(tt + 1) * 128], win[:, k, 256:768], start=(k == 0), stop=(k == 7))
        for k in range(8):
            P.mm(b2[:, 0:160], hT[:, k, tt * 128:(tt + 1) * 128], win[:, k, 1280:1440], start=(k == 0), stop=(k == 7))
        P.copy(self.F1[:, 0:512], b1[:, 0:512], eng=ACT)
        P.dma(self.ndk[seq, l, t0:t0 + 128, :], self.F1[:, 0:256])
        P.dma(self.ndv[seq, l, t0:t0 + 128, :], self.F1[:, 256:512])
        if self.stop_after == "d1":
            continue
        P.copy(VDP[:, tt, :, 0:64], b1[:, 256:512].rearrange("p (h c) -> p h c", c=64))
        if self.stop_after == "d2":
            continue
        P.act(self.F2[:, 0:128], b2[:, 0:128], AF.Square, accum_out=self.small[:, 8:9])
        P.act(self.small[:, 9:10], self.small[:, 8:9], AF.Ln, bias=self.eps_rms[:, 0:1], scale=1.0 / 128)
        P.act(self.small[:, 10:11], self.small[:, 9:10], AF.Exp, scale=-0.5)
        P.stt(self.F2[:, 128:256], b2[:, 0:128], self.small[:, 10:11], self.gkvb_s[:, :], ALU.mult, ALU.mult)
        P.dma(self.nckv[seq, l, t0:t0 + 128, :], self.F2[:, 128:256])
        if self.stop_after == "d3":
            continue
        P.copy(self.F2[:, 256:288], b2[:, 128:160])
        P.dma(self.nkr[seq, l, t0:t0 + 128, :], self.F2[:, 256:288])
    if self.stop_after in ("c3", "d1", "d2", "d3"):
        return
    for sb in range(2):
        t0 = NPR + sb * 512
        s0 = sb * 512
        n = 512
        cb = cosT[:, s0:s0 + n]
        snb = sinT[:, s0:s0 + n]

        def roped(ba, bb, rows=slice(0, 128)):
            st = self.stgbuf()
            P.tt(self.F1[rows, 0:n], ba[rows, 0:n], cb[rows], ALU.mult)
            P.tt(self.F2[rows, 0:n], bb[rows, 0:n], snb[rows], ALU.mult)
            P.tt(st[rows, 0:n], self.F1[rows, 0:n], self.F2[rows, 0:n], ALU.add)
            return st
        for j in range(2):
            ba = self.proj(lambda k, j=j: win[:, k, j * 128:(j + 1) * 128], t0, n)
            bb = self.proj(lambda k, j=j: winr[:, k, j * 128:(j + 1) * 128], t0, n)
            st = roped(ba, bb)
            for i in range(2):
                h = 2 * j + i
                P.dma(x1[h, 0:64, s0:s0 + n], st[i * 64:(i + 1) * 64, 0:n])
            ba = self.proj(lambda k, j=j: win[:, k, 256 + j * 128:256 + (j + 1) * 128], t0, n)
            bb = self.proj(lambda k, j=j: winr[:, k, 256 + j * 128:256 + (j + 1) * 128], t0, n)
            st = roped(ba, bb)
            for i in range(2):
                h = 2 * j + i
                P.dma(x1[h, 64:128, s0:s0 + n], st[i * 64:(i + 1) * 64, 0:n])
            ba = self.proj(lambda k, j=j: win[:, k, 768 + j * 128:768 + (j + 1) * 128], t0, n)
            st = self.stgbuf()
            P.copy(st[:, 0:n], ba[:, 0:n], eng=self.cpeng())
            for i in range(2):
                h = 2 * j + i
                P.dma(x1[h, 128:192, s0:s0 + n], st[i * 64:(i + 1) * 64, 0:n])
        if self.stop_after == "c4":
            return
        for tt in range(4):
            b1 = self.bank()
            for k in range(8):
                P.mm(b1[:, 0:256], hT[:, k, t0 + tt * 128:t0 + (tt + 1) * 128], win[:, k, 512:768],
                     start=(k == 0), stop=(k == 7))
            st = self.stgbuf()
            P.copy(st[:, 0:256], b1[:, 0:256], eng=self.cpeng())
            for h in range(4):
                if self.stop_after == "e1":
                    continue
                dst = x1[h, 192:256, :].rearrange("r (a b) -> (r a) b", b=64)
                P.dma(dst[s0 + tt * 128:s0 + (tt + 1) * 128, :], st[:, h * 64:(h + 1) * 64])
        if self.stop_after in ("e1", "e2"):
            return
        if self.stop_after == "c5":
            return
        bq = [self.proj(lambda k, j=j: win[:, k, 1024 + j * 128:1024 + (j + 1) * 128], t0, n) for j in range(2)]
        rstd = self.rms_fm(bq, n, 256)
        qn = self.QO
        for j in range(2):
            P.tt(qn[:, j, 0:n], bq[j][:, 0:n], rstd[:, 0:n], ALU.mult)
        for hq in range(8):
            ba = self.bank()
            bb = self.bank()
            for kt in range(2):
                P.mm(ba[0:96, 0:n], self.wuq[:, kt, hq * 96:(hq + 1) * 96], qn[:, kt, 0:n], start=(kt == 0), stop=(kt == 1))
            for kt in range(2):
                P.mm(bb[0:96, 0:n], self.wuqr[:, kt, hq * 96:(hq + 1) * 96], qn[:, kt, 0:n], start=(kt == 0), stop=(kt == 1))
            st = roped(ba, bb, rows=slice(64, 96))
            P.copy(st[0:64, 0:n], ba[0:64, 0:n], eng=self.cpeng())
            sh, w = hq // 2, hq % 2
            P.dma(x1[sh, 256 + w * 96:256 + (w + 1) * 96, s0:s0 + n], st[0:96, 0:n])
        bk = self.proj(lambda k: win[:, k, 1280:1408], t0, n)
        rstd = self.rms_fm([bk], n, 128)
        st = self.stgbuf()
        P.stt(st[:, 0:n], bk[:, 0:n], self.gkv_s[:, l:l + 1], rstd[:, 0:n], ALU.mult, ALU.mult)
        P.dma(g1[0:128, s0:s0 + n], st[:, 0:n])
        ba = self.proj(lambda k: wkr[:, k, :], t0, n, nrows=96)
        bb = self.proj(lambda k: wkrr[:, k, :], t0, n, nrows=96)
        st = roped(ba, bb, rows=slice(64, 96))
        P.dma(g1[128:160, s0:s0 + n], st[64:96, 0:n])


KB.stage0 = _stage0
KB.mods = _mods
KB.modulate = _modulate
KB.load_layer_weights = _load_layer_weights
KB.proj = _proj
KB.rms_fm = _rms_fm
KB.phaseC = _phaseC


def _step_side(self):
    side = getattr(self, "side", None)
    if side is not None:
        try:
            next(side)
        except StopIteration:
            self.side = None


def _attn_pass(self, specs, nkt, nq, scale):
    P = self.P
    LAG = 4 if len(specs) == 1 else 2
    for sp in specs:
        sp["sb"] = {}
        sp["pt"] = {}
    for step in range(nkt + LAG):
        if step < nkt:
            for sp in specs:
                sb = self.sbank()[:, 0:nq]
                sp["sb"][step] = sb
                kw = {}
                if sp.get("tp") is not None:
                    kw["tile_position"] = sp["tp"]
                P.mm(sb, sp["K"](step), sp["Q"], start=True, stop=True, **kw)
            for sp in specs:
                pt = self.ptbuf()[:, 0:nq]
                P.act(pt, sp["sb"][step], AF.Exp, scale=scale)
                sp["pt"][step] = pt
        if step >= LAG:
            kt = step - LAG
            for sp in specs:
                P.mm(sp["O"], sp["V"](kt), sp["pt"][kt], start=(kt == 0), stop=(kt == nkt - 1))
        if step % 16 == 15:
            self.step_side()


def _fin_mla(self, O, nq):
    P = self.P
    F1, F2 = self.F1, self.F2
    P.copy(F1[0:65, 0:nq], O[0:65, 0:nq])
    P.recip(F2[64:65, 0:nq], F1[64:65, 0:nq])
    bc = self.sbank()[0:64, 0:nq]
    P.mm(bc, self.ones_f[64:65, 0:64], F2[64:65, 0:nq])
    st = self.stgbuf()
    P.tt(st[0:64, 0:nq], F1[0:64, 0:nq], bc, ALU.mult)
    return st


def _fin_diff(self, O1, O2, nq, l):
    P = self.P
    F1, F2, F3, F4 = self.F1, self.F2, self.F3, self.F4
    P.copy(F1[0:65, 0:nq], O1[0:65, 0:nq])
    P.copy(F2[0:65, 0:nq], O2[0:65, 0:nq], eng=ACT)
    P.recip(F3[64:65, 0:nq], F1[64:65, 0:nq])
    P.recip(F4[64:65, 0:nq], F2[64:65, 0:nq])
    P.ts(F4[64:65, 0:nq], F4[64:65, 0:nq], self.lamt[64:65, l:l + 1], None, ALU.mult)
    bc = self.sbank()[0:64, 0:nq]
    P.mm(bc, self.ones_f[64:65, 0:64], F3[64:65, 0:nq])
    P.tt(F1[0:64, 0:nq], F1[0:64, 0:nq], bc, ALU.mult)
    bc = self.sbank()[0:64, 0:nq]
    P.mm(bc, self.ones_f[64:65, 0:64], F4[64:65, 0:nq])
    P.tt(F2[0:64, 0:nq], F2[0:64, 0:nq], bc, ALU.mult)
    P.tt(F1[0:64, 0:nq], F1[0:64, 0:nq], F2[0:64, 0:nq], ALU.subtract)
    P.act(self.SQB[0:64, 0, 0:nq], F1[0:64, 0:nq], AF.Square)
    bc = self.sbank()[0:64, 0:nq]
    P.mm(bc, self.ones_b[0:64, 0:64], self.SQB[0:64, 0, 0:nq])
    P.act(F3[0:64, 0:nq], bc, AF.Ln, bias=self.eps_rms[0:64, 0:1], scale=1.0 / 64)
    P.act(F4[0:64, 0:nq], F3[0:64, 0:nq], AF.Exp, scale=-0.5)
    st = self.stgbuf()
    P.stt(st[0:64, 0:nq], F1[0:64, 0:nq], self.gsc[:, l:l + 1], F4[0:64, 0:nq], ALU.mult, ALU.mult)
    return st


def _prompt_attention(self, l):
    P = self.P
    DQP = self.av(A_DQP, [128, 2, 512])
    KDP = self.av(A_KDP, [128, 2, 512])
    QNP = self.av(A_QNP, [128, 2, 512])
    CKP = self.av(A_CKP, [128, 512])
    KRP = self.av(A_KRP, [128, 512])
    VDP = self.av(A_VDP, [128, 4, 4, VW])
    KHP = self.av(A_KHP, [128, 2, 256])
    VHP = self.av(A_VHP, [128, 2, 2, VW])
    CAT = self.av(A_CAT, [128, 8, T])
    P.memset(VHP[:], 1.0)
    nq = 256
    cnt = 0
    for seq in range(2):
        cols = slice(seq * 256, (seq + 1) * 256)
        for h in range(4):
            j, rb = h // 2, (h % 2) * 64
            specs = []
            for c in range(2):
                r0 = rb + 32 * c
                specs.append(dict(
                    K=lambda kt, r0=r0, j=j, seq=seq: KDP[r0:r0 + 32, j, seq * 256 + kt * 128:seq * 256 + (kt + 1) * 128],
                    Q=DQP[r0:r0 + 32, j, cols],
                    V=lambda kt, h=h, seq=seq: VDP[:, seq * 2 + kt, h, 0:65],
                    O=self.B[4 + c][0:65, 0:nq], tp=(r0, 0)))
            self.attn_pass(specs, 2, nq, 32 ** -0.5)
            st = self.fin_diff(specs[0]["O"], specs[1]["O"], nq, l)
            P.dma(CAT[rb:rb + 64, j, cols], st[0:64, 0:nq])
            self.step_side()
        for h in range(8):
            buf = cnt % 2
            cnt += 1
            ps = self.bank()
            P.mm(ps[0:64, 0:nq], self.wuk[:, h * 64:(h + 1) * 64], CKP[:, cols])
            P.copy(KHP[0:64, buf, :], ps[0:64, 0:nq])
            P.copy(KHP[64:96, buf, :], KRP[64:96, cols])
            ps2 = self.bank()
            for kt in range(2):
                P.mm(ps2[:, kt * 64:(kt + 1) * 64], CKP[:, seq * 256 + kt * 128:seq * 256 + (kt + 1) * 128],
                     self.wuv[:, h * 64:(h + 1) * 64])
            P.copy(VHP[:, buf, :, 0:64], ps2[:, 0:128].rearrange("p (t c) -> p t c", c=64), eng=ACT)
            ps3 = self.bank()
            for kt in range(2):
                P.mm(ps3[0:96, 0:nq], self.wuq[:, kt, h * 96:(h + 1) * 96], QNP[:, kt, cols], start=(kt == 0), stop=(kt == 1))
            P.copy(self.QO[0:96, buf, 0:nq], ps3[0:96, 0:nq])
            O = self.B[4 + buf][0:65, 0:nq]
            specs = [dict(K=lambda kt, buf=buf: KHP[0:96, buf, kt * 128:(kt + 1) * 128],
                          Q=self.QO[0:96, buf, 0:nq],
                          V=lambda kt, buf=buf: VHP[:, buf, kt, 0:65], O=O, tp=None)]
            self.attn_pass(specs, 2, nq, 96 ** -0.5)
            st = self.fin_mla(O, nq)
            P.dma(CAT[(h % 2) * 64:(h % 2) * 64 + 64, 4 + h // 2, cols], st[0:64, 0:nq])
            self.step_side()


def _dyn_dma(self, out, src_fn, reads, writes, kind=0):
    def fn(e, out=out, src_fn=src_fn, kind=kind):
        if getattr(self, "_rank_val", None) is None:
            rk = e.partition_id() % 4
            self._rank_val = (e.snap(rk), e.snap(rk * NS))
        r = self._rank_val[kind]
        src = src_fn(r)
        try:
            return e.dma_start(out=out, in_=src)
        except Exception:
            raise
    op = Op(POOL, fn, "dyn")
    return self.P.dma_custom(op, reads=reads, writes=writes)


def _sample_loads(self, l):
    P = self.P
    eo = self.x1_out[l]
    xo4 = eo.rearrange("s (j w) t -> s j w t", j=4)
    g3 = self.g1_out[l].rearrange("(j w) t -> j w t", j=4)
    KDO = self.av(A_KDO, [128, NK])
    VDO = self.av(A_VDO, [128, NKT, VW])
    KH = self.av(A_KH, [128, 2, NK])
    VH = self.av(A_VH, [128, 2, NKT, VW])
    P.memset(VDO[:, :, 64:66], 1.0)
    P.memset(VH[:, :, :, 64:66], 1.0)
    o1 = self.own1[l]
    self.dyn_dma(o1, lambda r: xo4[ds(r, 1)].rearrange('o j w t -> (o j) w t'), reads=[eo], writes=[o1])
    P.dma(KDO[0:64, 0:4096].rearrange("w (j t) -> w j t", j=4), o1[:, 64:128, :].rearrange("j w t -> w j t"))
    for j in range(4):
        P.dma(VDO[:, j * 8:(j + 1) * 8, 0:64],
              o1[j, 192:256, :].rearrange("w (a b) -> (w a) b", b=64).rearrange("(t p) c -> p t c", p=128))
    for hh in range(2):
        P.dma(KH[64:96, hh, 0:4096].rearrange("w (j t) -> w j t", j=4),
              g3[:, 128:160, :].rearrange("j w t -> w j t"))
    ctmp = self.F1[:, 0:512].rearrange("p (i c) -> p i c", i=2)
    P.dma(ctmp[:, :, 0:64], self.c_dk[l].rearrange("(i p) c -> p i c", p=128))
    for i in range(2):
        ps = self.bank()
        P.transpose(ps[0:64, 0:128], ctmp[:, i, 0:64], self.ident_f[:])
        P.copy(KDO[0:64, 4096 + i * 128:4096 + (i + 1) * 128], ps[0:64, 0:128])
    P.dma(VDO[:, 32:34, 0:64], self.c_dv[l].rearrange("(i p) c -> p i c", p=128), eng=POOL)
    ktmp = self.F2[:, 0:192].rearrange("p (i c) -> p i c", i=2)
    P.memset(ktmp, 0.0)
    P.dma(ktmp[:, :, 64:96], self.c_kr[l].rearrange("(i p) c -> p i c", p=128))
    for i in range(2):
        ps = self.bank()
        P.transpose(ps[0:96, 0:128], ktmp[:, i, :], self.ident_f[:])
        for hh in range(2):
            P.copy(KH[64:96, hh, 4096 + i * 128:4096 + (i + 1) * 128], ps[64:96, 0:128])
    CKB = self.av(A_CKB, [128, 2, 512])
    ctmp2 = self.F3[:, 0:256].rearrange("p (i c) -> p i c", i=2)
    for kb in range(9):
        n = 512 if kb < 8 else 256
        cb = CKB[:, kb % 2, 0:n]
        if kb < 8:
            j, c0 = kb // 2, (kb % 2) * 512
            P.dma(cb, g3[j, 0:128, c0:c0 + 512])
        else:
            P.dma(ctmp2, self.c_ckv[l].rearrange("(i p) c -> p i c", p=128))
            for i in range(2):
                ps = self.bank()
                P.transpose(ps[:, 0:128], ctmp2[:, i, :], self.ident_f[:])
                P.copy(cb[:, i * 128:(i + 1) * 128], ps[:, 0:128])
        nt = n // 128
        psv = self.bank()
        for hh in range(2):
            ps = self.bank()
            P.mm(ps[0:64, 0:n], self.wuko[:, hh * 64:(hh + 1) * 64], cb)
            P.copy(KH[0:64, hh, kb * 512:kb * 512 + n], ps[0:64, 0:n], eng=self.cpeng())
        for t in range(nt):
            for hh in range(2):
                P.mm(psv[:, (t * 2 + hh) * 64:(t * 2 + hh + 1) * 64], cb[:, t * 128:(t + 1) * 128],
                     self.wuvo[:, hh * 64:(hh + 1) * 64])
        pv4 = psv[:, 0:nt * 128].rearrange("p (t h c) -> p t h c", h=2, c=64)
        for hh in range(2):
            P.copy(VH[:, hh, kb * 4:kb * 4 + nt, 0:64], pv4[:, :, hh, :], eng=self.cpeng())


def _sample_attention(self, l, side=None):
    P = self.P
    KDO = self.av(A_KDO, [128, NK])
    VDO = self.av(A_VDO, [128, NKT, VW])
    KH = self.av(A_KH, [128, 2, NK])
    VH = self.av(A_VH, [128, 2, NKT, VW])
    e2 = self.e2_in[l]
    QO = self.QO
    nq = 512

    step_side = self.step_side
    for qb in range(8):
        j, c0 = qb // 2, (qb % 2) * 512
        o1 = self.own1[l]
        P.dma(QO[0:64, 0, :], o1[j, 0:64, c0:c0 + 512])
        for hh in range(2):
            P.dma(QO[0:96, 1 + hh, :], o1[j, 256 + hh * 96:256 + (hh + 1) * 96, c0:c0 + 512])
        specs = []
        for c in range(2):
            r0 = 32 * c
            specs.append(dict(K=lambda kt, r0=r0: KDO[r0:r0 + 32, kt * 128:(kt + 1) * 128],
                              Q=QO[r0:r0 + 32, 0, :],
                              V=lambda kt: VDO[:, kt, 0:65],
                              O=self.B[4 + c][0:65, 0:nq], tp=(r0, 0)))
        self.attn_pass(specs, NKT, nq, 32 ** -0.5)
        st = self.fin_diff(specs[0]["O"], specs[1]["O"], nq, l)
        P.dma(e2[0:64, qb * 512:(qb + 1) * 512], st[0:64, 0:nq])
        step_side()
        for hh in range(2):
            O = self.B[4 + hh][0:65, 0:nq]
            specs = [dict(K=lambda kt, hh=hh: KH[0:96, hh, kt * 128:(kt + 1) * 128],
                          Q=QO[0:96, 1 + hh, :],
                          V=lambda kt, hh=hh: VH[:, hh, kt, 0:65], O=O, tp=None)]
            self.attn_pass(specs, NKT, nq, 96 ** -0.5)
            st = self.fin_mla(O, nq)
            P.dma(e2[64 + hh * 64:128 + hh * 64, qb * 512:(qb + 1) * 512], st[0:64, 0:nq])
            step_side()
    while self.side is not None:
        self.step_side()


KB.attn_pass = _attn_pass
KB.step_side = _step_side
KB.fin_mla = _fin_mla
KB.fin_diff = _fin_diff
KB.prompt_attention = _prompt_attention
KB.dyn_dma = _dyn_dma
KB.sample_loads = _sample_loads
KB.sample_attention = _sample_attention


TWO_PI = 2.0 * math.pi


def _rr(self, out, ang, shape):
    P = self.P
    n = shape[1]
    KI = self.bv(B_RR, [128, 256], F32).bitcast(I32)[0:shape[0], 0:n]
    P.ts(KI, ang, 1.0 / TWO_PI, None, ALU.mult)
    P.stt(out, KI, -TWO_PI, ang, ALU.mult, ALU.add)
    P.ts(out, out, -3.141592, 3.141592, ALU.max, ALU.min)


def _s5_prep(self, l, prm_src, b_src, c_src, nt, tile_ids, qs, mcols, h0_src=None):
    P = self.P
    sp = self.s5sm
    PR = self.s5prm
    for i, s in enumerate(tile_ids):
        pass
    for i, s in enumerate(tile_ids):
        P.dma(PR[:, :, i:i + 1], prm_src[:, :, s:s + 1], allow_slow_non_contiguous=True)
    lre, lim, ldt = PR[:, 0, 0:nt], PR[:, 1, 0:nt], PR[:, 2, 0:nt]
    f = lambda k: sp[:, k, 0:nt]
    DT, TH, MAG, S1, C1, ARE, AIM, DEN, NRE, FRE, FIM, T0, T1, CT, ST, NST = [f(k) for k in range(16)]
    P.act(DT, ldt, AF.Exp)
    P.tt(TH, lim, DT, ALU.mult)
    P.tt(T0, lre, DT, ALU.mult)
    P.act(MAG, T0, AF.Exp)
    self.rr(T0, TH, [128, nt])
    P.act(S1, T0, AF.Sin)
    P.ts(T1, TH, math.pi / 2, None, ALU.add)
    self.rr(T0, T1, [128, nt])
    P.act(C1, T0, AF.Sin)
    P.tt(ARE, MAG, C1, ALU.mult)
    P.tt(AIM, MAG, S1, ALU.mult)
    P.tt(DEN, lre, lre, ALU.mult)
    P.tt(T0, lim, lim, ALU.mult)
    P.tt(DEN, DEN, T0, ALU.add)
    P.recip(DEN, DEN)
    P.ts(NRE, ARE, -1.0, None, ALU.add)
    P.tt(T0, NRE, lre, ALU.mult)
    P.tt(T1, AIM, lim, ALU.mult)
    P.tt(T0, T0, T1, ALU.add)
    P.tt(FRE, T0, DEN, ALU.mult)
    P.tt(T0, AIM, lre, ALU.mult)
    P.tt(T1, NRE, lim, ALU.mult)
    P.tt(T0, T0, T1, ALU.subtract)
    P.tt(FIM, T0, DEN, ALU.mult)
    P.ts(T1, TH, 256.0, None, ALU.mult)
    self.rr(T0, T1, [128, nt])
    P.act(ST, T0, AF.Sin)
    P.ts(T1, T1, math.pi / 2, None, ALU.add)
    self.rr(T0, T1, [128, nt])
    P.act(CT, T0, AF.Sin)
    P.ts(NST, ST, -1.0, None, ALU.mult)
    P.ts(T1, FIM, -1.0, None, ALU.mult)
    NFIM = T1
    TAB = self.bv(B_TAB, [128, 8, 2, 256])
    CL = self.bv(B_CLHS, [128, 8, 2, 128])
    BL = self.bv(B_BLHS, [128, 2, 2, 128])
    BD = self.bv(B_BD, [128, 2, 128], F32)
    ANG = self.bv(B_RR + 512, [128, 256], F32)
    RED = self.bv(B_RR + 1024, [128, 256], F32)
    P.memset(CL[:], 0.0)
    for i, s in enumerate(tile_ids):
        q = qs[i]
        d = i // (nt // 2)
        P.ts(ANG, self.iota_f[:, 0:256], TH[:, i:i + 1], None, ALU.mult)
        self.rr(RED, ANG, [128, 256])
        P.act(TAB[:, i, 1, :], RED, AF.Sin)
        P.ts(ANG, ANG, math.pi / 2, None, ALU.add)
        self.rr(RED, ANG, [128, 256])
        P.act(TAB[:, i, 0, :], RED, AF.Sin)
        bt = self.s5bt
        P.dma(bt[:], b_src[s].rearrange("c p h -> p c h"))
        P.memset(BD[:], 0.0)
        for g2 in range(2):
            rows = slice(64 * g2, 64 * g2 + 64)
            cs = slice(32 * q + 16 * g2, 32 * q + 16 * g2 + 16)
            P.ts(self.s5t16[rows, :], bt[rows, 0, :], FRE[rows, i:i + 1], None, ALU.mult)
            P.stt(BD[rows, 0, cs], bt[rows, 1, :], NFIM[rows, i:i + 1], self.s5t16[rows, :], ALU.mult, ALU.add)
            P.ts(self.s5t16[rows, :], bt[rows, 1, :], FRE[rows, i:i + 1], None, ALU.mult)
            P.stt(BD[rows, 1, cs], bt[rows, 0, :], FIM[rows, i:i + 1], self.s5t16[rows, :], ALU.mult, ALU.add)
        for c in range(2):
            ps = self.B[6]
            P.transpose(ps[:, 0:128], BD[:, c, :], self.ident_f[:])
            P.copy(BL[32 * q:32 * q + 32, d, c, :], ps[32 * q:32 * q + 32, 0:128]) if False else None
            P.copy(self.blall[32 * q:32 * q + 32, i, c, :], ps[32 * q:32 * q + 32, 0:128])
        cd = self.s5cd
        P.memset(cd[:], 0.0)
        P.dma(cd[0:16, :, 0:64], c_src[s][:, 0:16, :].rearrange("c n p -> n c p"))
        P.dma(cd[16:32, :, 64:128], c_src[s][:, 16:32, :].rearrange("c n p -> n c p"))
        for c in range(2):
            ps = self.B[6]
            P.transpose(ps[:, 0:32], cd[:, c, :], self.ident_f[0:32, 0:32])
            if c == 0:
                P.copy(CL[:, i, c, 32 * q:32 * q + 32], ps[:, 0:32])
            else:
                P.ts(CL[:, i, c, 32 * q:32 * q + 32], ps[:, 0:32], -1.0, None, ALU.mult)
    return dict(MAG=MAG, CT=CT, ST=ST, NST=NST, TAB=TAB, CL=CL)


def _s5_flush(self):
    pend = getattr(self, "s5_pending", None)
    if pend:
        for f in pend:
            f()
    self.s5_pending = []


def _s5_bmm(self, i, q, u_rows):
    P = self.P
    self._bb_rr = getattr(self, "_bb_rr", 0) + 1
    bb = self.B[3] if self._bb_rr % 2 else self.B[6]
    P.mm(bb[:, 0:256], self.blall[32 * q:32 * q + 32, i, 0, :], u_rows, tile_position=(32 * q, 0))
    P.mm(bb[:, 256:512], self.blall[32 * q:32 * q + 32, i, 1, :], u_rows, tile_position=(32 * q, 0))
    return bb


def _s5_chunk(self, pr, i, q, u_rows, init, rev, ybank, ycols, first, last, mcols, bbank=None):
    P = self.P
    W = lambda k, dt=BF16: self.bv(B_WORK + k * 256, [128, 256], dt)
    XR, XI, HR, HI, T1, T2, T3, T4 = [W(k) for k in range(8)]
    self._h_rr = getattr(self, "_h_rr", 0) + 1
    if self._h_rr % 2:
        HR = self.av(27280, [128, 256])
        HI = self.av(27280 + 256, [128, 256])
    GR = self.bv(B_WORK + 2048, [128, 256], F32)
    GI = self.bv(B_WORK + 2560, [128, 256], F32)
    TAB = pr["TAB"]
    cos = TAB[:, i, 0, :]
    sin = TAB[:, i, 1, :]
    if rev:
        cos = cos[:, ::-1]
        sin = sin[:, ::-1]
    bb = self.s5_bmm(i, q, u_rows) if bbank is None else bbank
    bre, bim = bb[:, 0:256], bb[:, 256:512]
    P.tt(T1, bre, cos, ALU.mult)
    P.tt(T2, bim, sin, ALU.mult)
    P.tt(XR, T1, T2, ALU.add)
    P.tt(T3, bim, cos, ALU.mult)
    P.tt(T4, bre, sin, ALU.mult)
    P.tt(XI, T3, T4, ALU.subtract)
    magbc = pr["MAG"][:, i:i + 1].to_broadcast([128, 256])
    ir = init[0] if init is not None else 0.0
    ii = init[1] if init is not None else 0.0
    sl = (lambda a: a[:, ::-1]) if rev else (lambda a: a)
    P.scan(sl(GR), magbc, sl(XR), ir)
    P.scan(sl(GI), magbc, sl(XI), ii)
    PE_ = self.post_eng
    P.tt(T1, GR, cos, ALU.mult, eng=PE_)
    P.tt(T2, GI, sin, ALU.mult, eng=PE_)
    P.tt(HR, T1, T2, ALU.subtract, eng=PE_)
    P.tt(T3, GR, sin, ALU.mult, eng=PE_)
    P.tt(T4, GI, cos, ALU.mult, eng=PE_)
    P.tt(HI, T3, T4, ALU.add, eng=PE_)
    e = 0 if rev else 255
    st = self.s5st
    tmp = self.s5t16
    P.ts(tmp[:, 0:1], GR[:, e:e + 1], pr["CT"][:, i:i + 1], None, ALU.mult)
    P.ts(tmp[:, 1:2], GR[:, e:e + 1], pr["ST"][:, i:i + 1], None, ALU.mult)
    P.stt(st[:, i, 0:1], GI[:, e:e + 1], pr["NST"][:, i:i + 1], tmp[:, 0:1], ALU.mult, ALU.add)
    P.stt(st[:, i, 1:2], GI[:, e:e + 1], pr["CT"][:, i:i + 1], tmp[:, 1:2], ALU.mult, ALU.add)
    CL = pr["CL"]

    def cmm(HR=HR, HI=HI, i=i, first=first, last=last):
        P.mm(ybank[0:mcols, ycols], CL[:, i, 0, 0:mcols], HR, start=first, stop=False)
        P.mm(ybank[0:mcols, ycols], CL[:, i, 1, 0:mcols], HI, start=False, stop=last)
    self.s5_flush()
    self.s5_pending = [cmm]


def _s5_prompt(self, l):
    P = self.P
    UTP = self.av(A_UTP, [128, 2, 512])
    YP = self.av(A_YP, [128, 2, 512])
    for j in range(2):
        tids = [d * 8 + 4 * j + q for d in range(2) for q in range(4)]
        qs = [q for d in range(2) for q in range(4)]
        pr = self.s5_prep(l, self.s5p[l], self.s5b[l], self.s5c[l], 8, tids, qs, 128)
        yield
        P.ts(self.diagd[:], self.ident_b[:], self.s5d_s[:, l, j:j + 1], None, ALU.mult)
        for seq in range(2):
            cols = slice(seq * 256, (seq + 1) * 256)
            yb = self.B[7]
            self.s5_flush()
            P.mm(yb[:, 0:256], self.diagd[:], UTP[:, j, cols], start=True, stop=False)
            nxt = self.s5_bmm(0, 0, UTP[0:32, j, cols]) if self.prefetch else None
            for i in range(8):
                d, q = i // 4, i % 4
                cur = nxt
                if not self.prefetch:
                    cur = None
                elif i + 1 < 8:
                    q2 = (i + 1) % 4
                    nxt = self.s5_bmm(i + 1, q2, UTP[32 * q2:32 * q2 + 32, j, cols])
                self.s5_chunk(pr, i, q, UTP[32 * q:32 * q + 32, j, cols], None, d == 1, yb, slice(0, 256),
                              False, i == 7, 128, bbank=cur)
                gp = 4 * j + q
                P.dma(self.ns5[seq, l, d, 2 * gp:2 * gp + 2].rearrange("g p c -> (g p) c"), self.s5st[:, i, :])
                yield
            self.s5_flush()
            P.copy(YP[:, j, cols], yb[:, 0:256])


def _s5_sample(self, l):
    P = self.P
    tids = [0, 1, 2, 3]
    qs = [0, 1, 0, 1]
    pr = self.s5_prep(l, self.s5ps[l], self.s5bs[l], self.s5cs[l], 4, tids, qs, 64)
    UCH = self.bv(B_UCH, [128, 2, 256])
    YS = self.av(A_YS, [128, 4096])
    st = self.s5st
    for i in range(4):
        P.dma(st[:, i, :], self.s5h0[l, i])
    P.ts(self.diagd[0:64, 0:64], self.ident_b[0:64, 0:64], self.s5ds_s[:, l:l + 1], None, ALU.mult)
    yield
    cnt = 0
    for d in range(2):
        order = range(16) if d == 0 else range(15, -1, -1)
        for c in order:
            j, c0 = c // 4, (c % 4) * 256
            ub = UCH[0:64, cnt % 2, :]
            cnt += 1
            P.dma(ub, self.own1[l][j, 128:192, c0:c0 + 256])
            yb = self.B[7]
            self.s5_flush()
            if d == 1:
                P.mm(yb[0:64, 0:256], self.diagd[0:64, 0:64], ub, start=True, stop=False)
            bbs = [self.s5_bmm(d * 2 + gq, gq, ub[32 * gq:32 * gq + 32, :]) for gq in range(2)]
            for gq in range(2):
                i = d * 2 + gq
                self.s5_chunk(pr, i, gq, ub[32 * gq:32 * gq + 32, :], (st[:, i, 0:1], st[:, i, 1:2]), d == 1,
                              yb, slice(0, 256), (d == 0 and gq == 0), gq == 1, 64, bbank=bbs[gq])
            cs = slice(c * 256, (c + 1) * 256)

            def evac(d=d, cs=cs, yb=yb):
                if d == 0:
                    P.copy(YS[0:64, cs], yb[0:64, 0:256])
                else:
                    stg = self.stgbuf()
                    P.tt(stg[0:64, 0:256], yb[0:64, 0:256], YS[0:64, cs], ALU.add)
                    P.dma(self.e2_in[l][192:256, cs], stg[0:64, 0:256])
            self.s5_pending.append(evac)
            yield
    self.s5_flush()


def _s5_post(self, l, ysrc, t0, n):
    P = self.P
    CAT = self.av(A_CAT, [128, 8, T])
    Z = self.SQB
    for j in range(2):
        P.act(Z[:, j, 0:n], ysrc[:, j, :], AF.Gelu_apprx_tanh)
    for m in range(2):
        ps = self.bank()
        for k in range(2):
            P.mm(ps[:, 0:n], self.wglu[:, k, m * 128:(m + 1) * 128], Z[:, k, 0:n], start=(k == 0), stop=(k == 1))
        P.act(self.F1[:, 0:n], ps[:, 0:n], AF.Sigmoid)
        P.tt(CAT[:, 2 + m, t0:t0 + n], Z[:, m, 0:n], self.F1[:, 0:n], ALU.mult)


KB.rr = _rr
KB.s5_prep = _s5_prep
KB.s5_chunk = _s5_chunk
KB.s5_bmm = _s5_bmm
KB.s5_flush = _s5_flush
KB.s5_prompt = _s5_prompt
KB.s5_sample = _s5_sample
KB.s5_post = _s5_post


def _layer_norm(self, l, which, t0, n):
    P = self.P
    xT = self.xT
    F1, F2, F3, F4 = self.F1, self.F2, self.F3, self.F4
    s1, s2 = self.B[6], self.B[7]
    for k in range(8):
        st = self.stgbuf()
        P.copy(st[:, 0:n], xT[:, k, t0:t0 + n], eng=ACT)
        P.mm(s1[:, 0:n], self.ones_b[:], st[:, 0:n], start=(k == 0), stop=(k == 7))
        st2 = self.stgbuf()
        P.act(st2[:, 0:n], xT[:, k, t0:t0 + n], AF.Square)
        P.mm(s2[:, 0:n], self.ones_b[:], st2[:, 0:n], start=(k == 0), stop=(k == 7))
    P.ts(F1[:, 0:n], s1[:, 0:n], 1.0 / D, None, ALU.mult)
    P.tt(F2[:, 0:n], F1[:, 0:n], F1[:, 0:n], ALU.mult)
    P.stt(F3[:, 0:n], s2[:, 0:n], 1.0 / D, F2[:, 0:n], ALU.mult, ALU.subtract)
    P.act(F3[:, 0:n], F3[:, 0:n], AF.Ln, bias=self.eps_ln[:, 0:1], scale=1.0)
    P.act(F4[:, 0:n], F3[:, 0:n], AF.Exp, scale=-0.5)
    P.stt(F2[:, 0:n], F1[:, 0:n], -1.0, F4[:, 0:n], ALU.mult, ALU.mult)
    gi, bi = (0, 1) if which == 1 else (2, 3)
    for k in range(8):
        P.tt(F1[:, 0:n], xT[:, k, t0:t0 + n], F4[:, 0:n], ALU.mult)
        P.tt(F3[:, 0:n], F1[:, 0:n], F2[:, 0:n], ALU.add)
        P.ts(xT[:, k, t0:t0 + n], F3[:, 0:n], self.lnp_s[:, l, gi, k:k + 1], self.lnp_s[:, l, bi, k:k + 1],
             ALU.mult, ALU.add)


def _wout_ln1(self, l, t0, n, r):
    P = self.P
    CAT = self.av(A_CAT, [128, 8, T])
    WO = self.av(A_WOUT, [128, 8, D])
    for m in range(8):
        ps = self.bank()
        for k in range(8):
            P.mm(ps[:, 0:n], WO[:, k, m * 128:(m + 1) * 128], CAT[:, k, t0:t0 + n], start=(k == 0), stop=(k == 7))
        P.stt(self.xT[:, m, t0:t0 + n], ps[:, 0:n], self.MS[l][:, 16 + m, r:r + 1], self.xT[:, m, t0:t0 + n],
              ALU.mult, ALU.add)
    self.layer_norm(l, 1, t0, n)


def _ffn(self, l):
    P = self.P
    hT = self.modulate(l, 2)
    WGU = self.bv(B_WGU, [128, 3, 2, 8, 128])
    WD = self.bv(B_WD, [128, 2, NJ, 128])

    def actT(j):
        off = j * T if j < 10 else A_ACT + 15360 + (j - 10) * T
        off = A_ACT + j * T
        return self.av(off, [128, T])
    for j in range(NJ):
        if j == NJ - 3:
            for m0 in range(2):
                P.dma(WD[:, m0], self.w_down[l][:, m0 * 128:(m0 + 1) * 128].rearrange("(j p) n -> p j n", p=128), eng=POOL)
        buf = j % 3
        P.dma(WGU[:, buf, 0], self.w_gate[l][:, j * 128:(j + 1) * 128].rearrange("(k p) n -> p k n", p=128), eng=POOL)
        P.dma(WGU[:, buf, 1], self.w_up[l][:, j * 128:(j + 1) * 128].rearrange("(k p) n -> p k n", p=128), eng=POOL)
        for blk in range(3):
            t0 = blk * 512
            pg = self.bank()
            pu = self.bank()
            for k in range(8):
                P.mm(pg[:, 0:512], WGU[:, buf, 0, k, :], hT[:, k, t0:t0 + 512], start=(k == 0), stop=(k == 7))
            for k in range(8):
                P.mm(pu[:, 0:512], WGU[:, buf, 1, k, :], hT[:, k, t0:t0 + 512], start=(k == 0), stop=(k == 7))
            F = self.F1 if (j * 3 + blk) % 2 == 0 else self.F2
            P.act(F[:, 0:512], pg[:, 0:512], AF.Silu)
            P.tt(actT(j)[:, t0:t0 + 512], F[:, 0:512], pu[:, 0:512], ALU.mult)
    for m in range(8):
        buf = m % 2
        if m >= 2:
            P.dma(WD[:, buf], self.w_down[l][:, m * 128:(m + 1) * 128].rearrange("(j p) n -> p j n", p=128), eng=POOL)
        for blk in range(3):
            t0 = blk * 512
            r = 0 if blk == 0 else 1
            ps = self.bank()
            for j in range(NJ):
                P.mm(ps[:, 0:512], WD[:, buf, j, :], actT(j)[:, t0:t0 + 512], start=(j == 0), stop=(j == NJ - 1))
            P.stt(self.xT[:, m, t0:t0 + 512], ps[:, 0:512], self.MS[l][:, 40 + m, r:r + 1], self.xT[:, m, t0:t0 + 512],
                  ALU.mult, ALU.add)
    for blk in range(3):
        self.layer_norm(l, 2, blk * 512, 512)


def _output(self):
    P = self.P
    xo = self.av(A_XIN, [128, 2, D], F32)
    for tt in range(12):
        buf = xo[:, tt % 2, :]
        for half in range(2):
            bk = self.B[(tt * 2 + half) % 4]
            for kk in range(4):
                k = half * 4 + kk
                P.transpose(bk[:, kk * 128:(kk + 1) * 128], self.xT[:, k, tt * 128:(tt + 1) * 128], self.ident_f[:])
            P.copy(buf[:, half * 512:(half + 1) * 512], bk[:, :], eng=self.cpeng())
        P.dma(self.y_tok[tt * 128:(tt + 1) * 128, :], buf)


def _exchange(self, src, dst):
    def cc(e):
        return e.collective_compute("AllGather", ALU.bypass, replica_groups=self.groups, ins=[src], outs=[dst])
    return self.P.collective(cc, reads=[src], writes=[dst])


def _s5_alloc(self):
    s = self.P.sbuf
    self.s5sm = s("s5sm", [128, 16, 16], F32)
    self.s5prm = s("s5prm", [128, 3, 16], F32)
    self.s5bt = s("s5bt", [128, 2, 16], F32)
    self.s5t16 = s("s5t16", [128, 16], F32)
    self.s5cd = s("s5cd", [32, 2, 128], F32)
    self.s5st = s("s5st", [128, 8, 2], F32)
    self.blall = s("blall", [128, 8, 2, 128], BF16)
    self.diagd = s("diagd", [128, 128], BF16)
    self.s5d_s = s("s5d_s", [128, NL, 2], F32)
    self.s5ds_s = s("s5ds_s", [64, NL], F32)
    self.iota_f = s("iota_f", [128, 256], F32)
    P = self.P
    io = self.iota_f

    def fio(e):
        return e.iota(io[:], [[1, 256]], base=1, channel_multiplier=0, allow_small_or_imprecise_dtypes=True)
    P.add(POOL, fio, writes=[io[:]])
    P.dma(self.s5d_s[:], self.s5d)
    P.dma(self.s5ds_s[:], self.s5ds)


class StopBuild(Exception):
    pass


def _layer(self, l):
    P = self.P
    def chk(tag):
        if self.stop_after == tag:
            raise StopBuild()
    self.modulate(l, 1)
    ws = self.load_layer_weights(l)
    self.phaseC(l, *ws)
    chk("L_c")
    for sh in range(4):
        self.exchange(self.x1_in[l][sh], self.x1_out[l][sh])
    self.exchange(self.g1_in[l], self.g1_out[l])
    chk("L_x1")
    import itertools
    if l == 0:
        self.mods(0, part=1)
    if l + 1 < NL:
        self.mods(l + 1)
    self.side = itertools.chain(self.s5_prompt(l), self.s5_sample(l))
    self.prompt_attention(l)
    chk("L_pa")
    self.sample_loads(l)
    chk("L_sl")
    self.sample_attention(l)
    chk("L_sa")
    for hf in range(2):
        self.exchange(self.e2_in[l][hf * 128:(hf + 1) * 128, :], self.e2_out[l][hf])
    chk("L_x2")
    CAT = self.av(A_CAT, [128, 8, T])
    YST = self.av(A_YST, [128, 2, NS])
    YP = self.av(A_YP, [128, 2, 512])
    eo = self.e2_out[l]
    e4 = eo.rearrange("h (j w) t -> h j w t", j=4)
    o2 = self.own2[l]
    for hf in range(2):
        self.dyn_dma(o2[:, hf * 128:(hf + 1) * 128, :], lambda r, hf=hf: e4[hf, :, :, ds(r, NS)], reads=[eo],
                     writes=[o2[:, hf * 128:(hf + 1) * 128, :]], kind=1)
    for j in range(4):
        half = slice((j % 2) * 64, (j % 2) * 64 + 64)
        P.dma(CAT[half, j // 2, NPR:T], o2[j, 0:64, :])
        P.dma(CAT[:, 4 + j, NPR:T], o2[j, 64:192, :])
        P.dma(YST[half, j // 2, :], o2[j, 192:256, :])
    WO = self.av(A_WOUT, [128, 8, D])
    P.dma(WO, self.w_out[l].rearrange("(k p) n -> p k n", p=128), eng=POOL)
    self.s5_post(l, YP[:, :, :], 0, 512)
    self.wout_ln1(l, 0, 512, 0)
    for sb in range(2):
        self.s5_post(l, YST[:, :, sb * 512:(sb + 1) * 512], NPR + sb * 512, 512)
        self.wout_ln1(l, NPR + sb * 512, 512, 1)
    self.ffn(l)


def build_all(ncores=8, stop_after=None):
    kb = KB(stop_after=stop_after)
    kb.groups = [[0, 1, 2, 3], [4, 5, 6, 7]] if ncores == 8 else [[0, 1, 2, 3]]
    kb.declare()
    kb.s5_alloc()
    kb.stage0()
    try:
        for l in range(NL):
            kb.layer(l)
        kb.output()
    except StopBuild:
        pass
    return kb


KB.layer_norm = _layer_norm
KB.wout_ln1 = _wout_ln1
KB.ffn = _ffn
KB.output = _output
KB.exchange = _exchange
KB.s5_alloc = _s5_alloc
KB.layer = _layer


def rope_tables(r):
    t = np.arange(1024) + r * 1024
    row = (t // 64).astype(np.float32)
    col = (t % 64).astype(np.float32)
    n_freq = 8
    inv = (np.float32(10000.0) ** (-np.arange(n_freq, dtype=np.float32) / np.float32(n_freq))).astype(np.float32)
    ang = np.concatenate([row[:, None] * inv, col[:, None] * inv], -1).astype(np.float32)
    cos = np.cos(ang).astype(np.float32).T
    sin = np.sin(ang).astype(np.float32).T
    c32 = np.concatenate([cos, cos], 0)
    s32 = np.concatenate([sin, sin], 0)
    return np.stack([np.tile(c32, (4, 1)), np.tile(s32, (4, 1))], 0).astype(np.float32)

def tile_rows(a):
    sh = a.shape[:-1]
    return np.ascontiguousarray(np.moveaxis(a.reshape(sh + (a.shape[-1] // 128, 128)), -1, 0))

def make_in_maps(inp):
    f = lambda a: np.ascontiguousarray(a, dtype=np.float32)
    shared = {}
    shared["w_ada"] = f(inp["w_ada"])
    b = inp["b_ada"].reshape(NL, 48, 128)
    shared["badaT"] = f(np.transpose(b, (2, 0, 1)))
    for k in ["w_in", "w_out", "w_gate", "w_up", "w_down", "w_glu", "w_uq", "w_uk", "w_uv"]:
        src = {"w_gate": "ffn_w_gate", "w_up": "ffn_w_up", "w_down": "ffn_w_down", "w_glu": "s5_w_glu",
               "w_uq": "mla_w_uq", "w_uk": "mla_w_uk", "w_uv": "mla_w_uv"}.get(k, k)
        shared[k] = f(inp[src])
    lqk = np.concatenate([inp["diff_lq1"], inp["diff_lk1"], inp["diff_lq2"], inp["diff_lk2"]], -1)
    shared["lqk"] = f(np.broadcast_to(lqk[None], (128, NL, 128)))
    shared["dng"] = f(inp["diff_norm_g"].T)
    shared["gq"] = f(np.transpose(inp["mla_q_norm_g"].reshape(NL, 2, 128), (2, 0, 1)))
    shared["gkv"] = f(inp["mla_kv_norm_g"].T)
    shared["gkv_b"] = f(np.broadcast_to(inp["mla_kv_norm_g"][:, None, :], (NL, 128, 128)))
    ln = np.stack([inp["ln1_g"], inp["ln1_b"], inp["ln2_g"], inp["ln2_b"]], 1)
    shared["lnp"] = f(np.transpose(ln.reshape(NL, 4, 8, 128), (3, 0, 1, 2)))
    logdt = np.broadcast_to(inp["s5_log_dt"][..., None], inp["s5_lam_re"].shape)
    prm = np.stack([inp["s5_lam_re"], inp["s5_lam_im"], logdt], 0)
    prm = prm.reshape(3, NL, 2, 8, 2, 64)
    shared["s5p"] = f(np.transpose(prm, (1, 4, 5, 0, 2, 3)).reshape(NL, 128, 3, 16))
    bb = np.stack([inp["s5_b_re"], inp["s5_b_im"]], 0)
    bb = bb.reshape(2, NL, 2, 8, 2, 64, 16)
    shared["s5b"] = f(np.transpose(bb, (1, 2, 3, 0, 4, 5, 6)).reshape(NL, 16, 2, 128, 16))
    cc = np.stack([inp["s5_c_re"], inp["s5_c_im"]], 0)
    cc = cc.reshape(2, NL, 2, 8, 2, 16, 64)
    shared["s5c"] = f(np.transpose(cc, (1, 2, 3, 0, 4, 5, 6)).reshape(NL, 16, 2, 32, 64))
    dd = inp["s5_d"].reshape(NL, 2, 128)
    shared["s5d"] = f(np.transpose(dd, (2, 0, 1)))
    shared["identd"] = np.eye(128, dtype=np.float32)
    maps = []
    for c in range(8):
        r, b_ = c % 4, c // 4
        m = dict(shared)
        xp = inp["x_prompt"][2 * c:2 * c + 2].reshape(512, 1024)
        xs = inp["x_sample"][b_, r * 1024:(r + 1) * 1024]
        m["x_tok"] = f(np.concatenate([xp, xs], 0))
        cond = np.stack([inp["c_ctx"], inp["c"][b_]], 0)
        m["condT"] = f(np.transpose(cond.reshape(2, 8, 128), (2, 1, 0)))
        m["w_uk_own"] = f(inp["mla_w_uk"][:, :, r * 128:(r + 1) * 128])
        m["w_uv_own"] = f(inp["mla_w_uv"][:, :, r * 128:(r + 1) * 128])
        prs = prm[:, :, :, 2 * r:2 * r + 2]
        m["s5ps"] = f(np.transpose(prs, (1, 4, 5, 0, 2, 3)).reshape(NL, 128, 3, 4))
        bs = bb[:, :, :, 2 * r:2 * r + 2]
        m["s5bs"] = f(np.transpose(bs, (1, 2, 3, 0, 4, 5, 6)).reshape(NL, 4, 2, 128, 16))
        cs = cc[:, :, :, 2 * r:2 * r + 2]
        m["s5cs"] = f(np.transpose(cs, (1, 2, 3, 0, 4, 5, 6)).reshape(NL, 4, 2, 32, 64))
        m["s5ds"] = f(inp["s5_d"][:, 4 * r:4 * r + 4].reshape(NL, 64).T)
        h0 = inp["state_s5"][b_][:, :, 4 * r:4 * r + 4]
        m["s5h0"] = f(h0.reshape(NL, 2, 2, 128, 2).reshape(NL, 4, 128, 2))
        m["c_dk"] = f(inp["cache_diff_k"][b_][:, :, r, :])
        m["c_dv"] = f(inp["cache_diff_v"][b_][:, :, r, :])
        m["c_ckv"] = f(inp["cache_mla_ckv"][b_])
        m["c_kr"] = f(inp["cache_mla_krope"][b_])
        m["rope"] = rope_tables(r)
        maps.append(m)
    return maps


_CACHE = {}


def kernel(**inputs):
    inp = {k: np.asarray(v) for k, v in inputs.items()}
    if "kb" not in _CACHE:
        kb = build_all(8)
        kb.P.emit()
        _CACHE["kb"] = kb
    kb = _CACHE["kb"]
    maps = make_in_maps(inp)
    used = set(kb.ins.keys())
    maps = [{k: v for k, v in m.items() if k in used} for m in maps]
    res = run_bass_kernel_spmd(kb.nc, maps, core_ids=list(range(8)))
    R = res.results
    y_prompt = np.zeros((16, 256, 1024), np.float32)
    y_sample = np.zeros((2, 4096, 1024), np.float32)
    ndk = np.zeros((16, 2, 256, 4, 64), np.float32)
    ndv = np.zeros((16, 2, 256, 4, 64), np.float32)
    nckv = np.zeros((16, 2, 256, 128), np.float32)
    nkr = np.zeros((16, 2, 256, 32), np.float32)
    ns5 = np.zeros((16, 2, 2, 16, 64, 2), np.float32)
    for c in range(8):
        r, b = c % 4, c // 4
        o = R[c]
        y_prompt[2 * c:2 * c + 2] = np.asarray(o["y_tok"][:512]).reshape(2, 256, 1024)
        y_sample[b, r * 1024:(r + 1) * 1024] = np.asarray(o["y_tok"][512:])
        ndk[2 * c:2 * c + 2] = np.asarray(o["ndk"]).reshape(2, 2, 256, 4, 64)
        ndv[2 * c:2 * c + 2] = np.asarray(o["ndv"]).reshape(2, 2, 256, 4, 64)
        nckv[2 * c:2 * c + 2] = np.asarray(o["nckv"])
        nkr[2 * c:2 * c + 2] = np.asarray(o["nkr"])
        ns5[2 * c:2 * c + 2] = np.asarray(o["ns5"])
    return (y_prompt, y_sample, ndk, ndv, nckv, nkr, ns5)
```

```python
import math
import numpy as np
from concourse.bass_utils import run_bass_kernel_spmd
import contextlib
import concourse.bass as bass
import concourse.mybir as mybir

F32 = mybir.dt.float32
BF16 = mybir.dt.bfloat16
I32 = mybir.dt.int32
AF = mybir.ActivationFunctionType
ALU = mybir.AluOpType

PE, ACT, DVE, POOL, SP = "pe", "act", "dve", "pool", "sp"
ENGINES = (PE, ACT, DVE, POOL, SP)
N_DMA_SEMS = 48


def region_of(ap):
    t = ap.tensor
    name = t.name
    dims = [(int(s), int(c)) for s, c in ap.ap]
    off = int(ap.offset)
    space = str(ap.space) if hasattr(ap, "space") else ""
    shape = [int(s) for s in t.shape]
    if "DRAM" in space.upper() or "dram" in type(t).__name__.lower() or "DRam" in type(t).__name__:
        lo = off + sum(min(0, s * (c - 1)) for s, c in dims)
        hi = off + sum(max(0, s * (c - 1)) for s, c in dims) + 1
        return (name, 0, 1, lo, hi)
    if "PSum" in type(t).__name__:
        return ("%PSUM%" + name, 0, 128, 0, 1 << 30)
    fsz = 1
    for s in shape[1:]:
        fsz *= s
    p0 = off // fsz
    f0 = off % fsz
    if dims and dims[0][0] == fsz:
        npart = dims[0][1]
        rest = dims[1:]
    elif dims and dims[0][0] == 0 and len(dims) > 1:
        npart = 1
        rest = dims[1:]
    else:
        npart = 1
        rest = dims
    lo = f0 + sum(min(0, s * (c - 1)) for s, c in rest)
    hi = f0 + sum(max(0, s * (c - 1)) for s, c in rest) + 1
    return (name, p0, p0 + npart, lo, hi)


def overlap(a, b):
    return a[1] < b[2] and b[1] < a[2] and a[3] < b[4] and b[3] < a[4]


class Op:
    __slots__ = ("eng", "fn", "idx", "waits", "signal", "dma_sem", "dma_val", "is_dma", "name", "sigval")

    def __init__(self, eng, fn, name=""):
        self.eng = eng
        self.fn = fn
        self.idx = None
        self.waits = {}
        self.signal = False
        self.is_dma = False
        self.dma_sem = None
        self.dma_val = None
        self.name = name


class Prog:
    def __init__(self, nc):
        self.nc = nc
        self.ops = {e: [] for e in ENGINES}
        self.recs = {}
        self.dma_count = 0
        self.dma_count_sw = 0
        self.dma_last = [None] * N_DMA_SEMS
        self.stack = contextlib.ExitStack()
        self.n_ops = 0

    def sbuf(self, name, shape, dtype):
        return self.stack.enter_context(self.nc.sbuf_tensor(name, list(shape), dtype))

    def psum(self, name, shape, dtype=F32):
        return self.stack.enter_context(self.nc.psum_tensor(name, list(shape), dtype))

    def _deps(self, op, reads, writes, pe_accum=False):
        deps = []
        rr = [region_of(a) for a in reads]
        ww = [region_of(a) for a in writes]
        for reg, kind in [(r, "r") for r in rr] + [(w, "w") for w in ww]:
            lst = self.recs.setdefault(reg[0], [])
            for rec in lst:
                oreg, okind, oop = rec
                if okind == "r" and kind == "r":
                    if not (reg[0].startswith("%PSUM%") and oop.eng != op.eng):
                        continue
                if not overlap(reg, oreg):
                    continue
                if oop is op:
                    continue
                if oop.eng == op.eng and not oop.is_dma and not op.is_dma:
                    if op.eng == PE:
                        continue
                deps.append(oop)
        for reg, kind in [(r, "r") for r in rr] + [(w, "w") for w in ww]:
            lst = self.recs.setdefault(reg[0], [])
            new = []
            for rec in lst:
                oreg, okind, oop = rec
                if kind == "w" and (not reg[0].startswith("%PSUM%")) and oreg[1] >= reg[1] and oreg[2] <= reg[2] and oreg[3] >= reg[3] and oreg[4] <= reg[4]:
                    continue
                if reg[0].startswith("%PSUM%") and oop.eng == op.eng and not op.is_dma:
                    continue
                if oop.eng == op.eng and (not oop.is_dma) and (not op.is_dma) and okind == kind and oreg == reg:
                    continue
                new.append(rec)
            new.append((reg, kind, op))
            self.recs[reg[0]] = new
        return deps

    def _add_waits(self, op, deps):
        for d in deps:
            if d.is_dma:
                key = ("dma", d.dma_sem)
                val = d.dma_val
            else:
                d.signal = True
                key = d.eng
                val = d
            cur = op.waits.get(key)
            if cur is None:
                op.waits[key] = val
            else:
                if d.is_dma:
                    op.waits[key] = max(cur, val)
                else:
                    op.waits[key] = cur if cur.idx >= d.idx else d

    def add(self, eng, fn, reads=(), writes=(), name=""):
        op = Op(eng, fn, name)
        op.idx = len(self.ops[eng])
        deps = self._deps(op, reads, writes)
        self._add_waits(op, deps)
        self.ops[eng].append(op)
        self.n_ops += 1
        return op

    def _dma_slot(self, op, eng):
        half = N_DMA_SEMS // 2
        if eng == POOL:
            c = self.dma_count_sw
            self.dma_count_sw += 1
            k = half + c % half
        else:
            c = self.dma_count
            self.dma_count += 1
            k = c % half
        op.dma_sem = k
        op.dma_val = 16 * (c // half + 1)
        return k

    def dma(self, out, in_, eng=SP, extra_reads=(), extra_writes=(), **kw):
        def fn(e, out=out, in_=in_, kw=kw):
            return e.dma_start(out=out, in_=in_, **kw)
        op = Op(eng, fn, "dma")
        op.is_dma = True
        op.idx = len(self.ops[eng])
        k = self._dma_slot(op, eng)
        prev = self.dma_last[k]
        deps = self._deps(op, [in_] + list(extra_reads), [out] + list(extra_writes))
        if prev is not None:
            deps.append(prev)
        self.dma_last[k] = op
        self._add_waits(op, deps)
        self.ops[eng].append(op)
        self.n_ops += 1
        return op

    def dma_custom(self, op, reads, writes):
        op.is_dma = True
        eng = op.eng
        op.idx = len(self.ops[eng])
        k = self._dma_slot(op, eng)
        prev = self.dma_last[k]
        deps = self._deps(op, list(reads), list(writes))
        if prev is not None:
            deps.append(prev)
        self.dma_last[k] = op
        self._add_waits(op, deps)
        self.ops[eng].append(op)
        self.n_ops += 1
        return op

    def collective(self, fn, reads, writes):
        op = Op(POOL, fn, "cc")
        op.is_dma = True
        op.idx = len(self.ops[POOL])
        if not hasattr(self, "n_cc"):
            self.n_cc = 0
        op.dma_sem = "cc%d" % self.n_cc
        self.n_cc += 1
        op.dma_val = 1
        deps = self._deps(op, reads, writes)
        self._add_waits(op, deps)
        self.ops[POOL].append(op)
        self.n_ops += 1
        return op

    def mm(self, out, lhsT, rhs, start=True, stop=True, **kw):
        def fn(e):
            return e.matmul(out, lhsT, rhs, start=start, stop=stop, **kw)
        return self.add(PE, fn, reads=[lhsT, rhs], writes=[out], name="mm")

    def transpose(self, out, in_, ident):
        def fn(e):
            return e.transpose(out, in_, ident)
        return self.add(PE, fn, reads=[in_, ident], writes=[out], name="tr")

    def act(self, out, in_, func, bias=None, scale=None, accum_out=None, eng=ACT):
        kw = {}
        reads = [in_]
        if bias is not None:
            kw["bias"] = bias
            if not isinstance(bias, (int, float)):
                reads.append(bias)
        if scale is not None:
            kw["scale"] = scale
            if not isinstance(scale, (int, float)):
                reads.append(scale)
        writes = [out]
        if accum_out is not None:
            kw["accum_out"] = accum_out
            writes.append(accum_out)

        def fn(e):
            return e.activation(out, in_, func, **kw)
        return self.add(eng, fn, reads=reads, writes=writes, name="act")

    def tt(self, out, in0, in1, op, eng=DVE):
        def fn(e):
            return e.tensor_tensor(out, in0, in1, op)
        return self.add(eng, fn, reads=[in0, in1], writes=[out], name="tt")

    def ts(self, out, in0, s1, s2, op0, op1=None, eng=DVE, accum_out=None):
        reads = [in0]
        if s1 is not None and not isinstance(s1, (int, float)):
            reads.append(s1)
        if s2 is not None and not isinstance(s2, (int, float)):
            reads.append(s2)
        writes = [out]
        kw = {}
        if accum_out is not None:
            kw["accum_out"] = accum_out
            writes.append(accum_out)

        def fn(e):
            if op1 is None:
                return e.tensor_scalar(out, in0, s1, None, op0, **kw)
            return e.tensor_scalar(out, in0, s1, s2, op0, op1, **kw)
        return self.add(eng, fn, reads=reads, writes=writes, name="ts")

    def stt(self, out, in0, scalar, in1, op0, op1, eng=DVE):
        reads = [in0, in1]
        if not isinstance(scalar, (int, float)):
            reads.append(scalar)

        def fn(e):
            return e.scalar_tensor_tensor(out, in0, scalar, in1, op0, op1)
        return self.add(eng, fn, reads=reads, writes=[out], name="stt")

    def scan(self, out, d0, d1, initial, op0=ALU.mult, op1=ALU.add):
        reads = [d0, d1]
        if not isinstance(initial, (int, float)):
            reads.append(initial)

        def fn(e):
            return e.tensor_tensor_scan(out, d0, d1, initial, op0, op1)
        return self.add(DVE, fn, reads=reads, writes=[out], name="scan")

    def copy(self, out, in_, eng=DVE):
        if eng == ACT:
            def fn(e):
                return e.copy(out, in_)
        else:
            def fn(e):
                return e.tensor_copy(out, in_)
        return self.add(eng, fn, reads=[in_], writes=[out], name="copy")

    def memset(self, ap, val, eng=DVE):
        def fn(e):
            return e.memset(ap, val)
        return self.add(eng, fn, reads=[], writes=[ap], name="memset")

    def recip(self, out, in_):
        def fn(e):
            return e.reciprocal(out, in_)
        return self.add(DVE, fn, reads=[in_], writes=[out], name="recip")

    def recip_fast(self, out, in_):
        def fn(e):
            return e.reciprocal_approx_fast(out, in_)
        return self.add(DVE, fn, reads=[in_], writes=[out], name="recipf")

    def emit(self, final_wait_all=True):
        nc = self.nc
        for e in ENGINES:
            cnt = 0
            for op in self.ops[e]:
                if op.is_dma:
                    continue
                if op.signal:
                    cnt += 1
                    op.sigval = cnt
        sems = {}
        st = self.stack
        for e in ENGINES:
            sems[e] = st.enter_context(nc.semaphore("sem_" + e))
        dsems = [st.enter_context(nc.semaphore("dsem%d" % i)) for i in range(N_DMA_SEMS)]
        ccsems = {"cc%d" % i: st.enter_context(nc.semaphore("ccsem%d" % i)) for i in range(getattr(self, "n_cc", 0))}
        engobj = {PE: "tensor", ACT: "scalar", DVE: "vector", POOL: "gpsimd", SP: "sync"}
        prog = self

        def run_engine(ename, eng):
            waited = {}
            for op in prog.ops[ename]:
                for key, val in op.waits.items():
                    if isinstance(key, tuple):
                        sem = ccsems[key[1]] if isinstance(key[1], str) else dsems[key[1]]
                        v = val
                    else:
                        sem = sems[key]
                        v = val.sigval
                    if waited.get(key, 0) >= v:
                        continue
                    waited[key] = v
                    eng.wait_ge(sem, v)
                ins = op.fn(eng)
                if op.is_dma and isinstance(op.dma_sem, str):
                    ins.then_inc(ccsems[op.dma_sem], 1)
                elif op.is_dma:
                    ins.then_inc(dsems[op.dma_sem], 16)
                elif op.signal:
                    ins.then_inc(sems[ename], 1)
            if ename == SP and final_wait_all:
                for k in range(N_DMA_SEMS):
                    last = prog.dma_last[k]
                    if last is not None and waited.get(("dma", k), 0) < last.dma_val:
                        eng.wait_ge(dsems[k], last.dma_val)

        with nc.Block() as block:
            @block.tensor
            def _(e):
                run_engine(PE, e)

            @block.scalar
            def _(e):
                run_engine(ACT, e)

            @block.vector
            def _(e):
                run_engine(DVE, e)

            @block.gpsimd
            def _(e):
                run_engine(POOL, e)

            @block.sync
            def _(e):
                run_engine(SP, e)
        self.stack.close()


from concourse.bass import ds

NL = 2
D = 1024
T = 1536
NPR = 512
NS = 1024
NK = 4352
NKT = 34
ALPHA = (2 * NL) ** 0.25
LN_EPS = 1e-5
RMS_EPS = 1e-6
FFN_H = 2816
NJ = 22
AX = mybir.AxisListType
GRP = [[0, 1, 2, 3], [4, 5, 6, 7]]
SH = 448
E1R = 4 * SH + 160


def lam_init(l):
    return 0.8 - 0.6 * math.exp(-0.3 * l)


class KB:
    def __init__(self, stop_after=None, dbg=False):
        self.stop_after = stop_after
        self.dbg = dbg
        self.nc = bass.Bass("TRN2", target_bir_lowering=False)
        self.P = Prog(self.nc)
        self.ins = {}
        self.outs = {}
        self._bank_rr = 0
        self._s_rr = 0
        self._pt_rr = 0
        self._stg_rr = 0
        self._cp_rr = 0
        import os
        self.post_eng = POOL if os.environ.get('POST_POOL', '0') == '1' else DVE
        self.prefetch = os.environ.get('PREFETCH', '1') == '1'

    def din(self, name, shape, dt=F32):
        ap = self.nc.dram_tensor(name, list(shape), dt, kind="ExternalInput").ap()
        self.ins[name] = ap
        return ap

    def dout(self, name, shape, dt=F32):
        ap = self.nc.dram_tensor(name, list(shape), dt, kind="ExternalOutput").ap()
        self.outs[name] = ap
        return ap

    def dint(self, name, shape, dt=BF16):
        return self.nc.dram_tensor(name, list(shape), dt, kind="Internal").ap()

    def av(self, off, shape, dtype=BF16):
        n = 1
        for s in shape[1:]:
            n *= s
        if dtype == F32:
            n *= 2
        v = self.ARA[0:shape[0], off:off + n]
        if dtype == F32:
            v = v.bitcast(F32)
        if len(shape) == 3:
            v = v.rearrange("p (a b) -> p a b", b=shape[2])
        elif len(shape) == 4:
            v = v.rearrange("p (a b c) -> p a b c", b=shape[2], c=shape[3])
        elif len(shape) == 5:
            v = v.rearrange("p (a b c d) -> p a b c d", b=shape[2], c=shape[3], d=shape[4])
        return v

    def bv(self, off, shape, dtype=BF16):
        n = 1
        for s in shape[1:]:
            n *= s
        if dtype == F32:
            n *= 2
        v = self.ARB[0:shape[0], off:off + n]
        if dtype == F32:
            v = v.bitcast(F32)
        if len(shape) == 3:
            v = v.rearrange("p (a b) -> p a b", b=shape[2])
        elif len(shape) == 4:
            v = v.rearrange("p (a b c) -> p a b c", b=shape[2], c=shape[3])
        elif len(shape) == 5:
            v = v.rearrange("p (a b c d) -> p a b c d", b=shape[2], c=shape[3], d=shape[4])
        return v

    def bank(self):
        b = self.B[(0, 1, 2, 4, 5)[self._bank_rr % 5]]
        self._bank_rr += 1
        return b

    def sbank(self):
        b = self.B[self._s_rr % 3]
        self._s_rr += 1
        return b

    def ptbuf(self):
        b = self.PT[:, self._pt_rr % 6, :]
        self._pt_rr += 1
        return b

    def stgbuf(self):
        b = self.STG[:, self._stg_rr % 4, :]
        self._stg_rr += 1
        return b

    def cpeng(self):
        self._cp_rr += 1
        return DVE if self._cp_rr % 2 else ACT

    def reduce_sum(self, out, in_):
        def fn(e):
            return e.tensor_reduce(out, in_, AX.X, ALU.add)
        return self.P.add(DVE, fn, reads=[in_], writes=[out], name="red")

    def declare(self):
        P = self.P
        d = self.din
        self.x_tok = d("x_tok", [T, D])
        self.condT = d("condT", [128, 8, 2])
        self.w_ada = d("w_ada", [NL, D, 6 * D])
        self.badaT = d("badaT", [128, NL, 48])
        self.w_in = d("w_in", [NL, D, 1440])
        self.w_out = d("w_out", [NL, D, D])
        self.lqk = d("lqk", [128, NL, 128])
        self.dng = d("dng", [64, NL])
        self.gq = d("gq", [128, NL, 2])
        self.gkv = d("gkv", [128, NL])
        self.gkv_b = d("gkv_b", [NL, 128, 128])
        self.w_uq = d("w_uq", [NL, 256, 768])
        self.w_uk = d("w_uk", [NL, 128, 512])
        self.w_uk_own = d("w_uk_own", [NL, 128, 128])
        self.w_uv = d("w_uv", [NL, 128, 512])
        self.w_uv_own = d("w_uv_own", [NL, 128, 128])
        self.lnp = d("lnp", [128, NL, 4, 8])
        self.w_gate = d("w_gate", [NL, D, FFN_H])
        self.w_up = d("w_up", [NL, D, FFN_H])
        self.w_down = d("w_down", [NL, FFN_H, D])
        self.s5p = d("s5p", [NL, 128, 3, 16])
        self.s5ps = d("s5ps", [NL, 128, 3, 4])
        self.s5b = d("s5b", [NL, 16, 2, 128, 16])
        self.s5bs = d("s5bs", [NL, 4, 2, 128, 16])
        self.s5c = d("s5c", [NL, 16, 2, 32, 64])
        self.s5cs = d("s5cs", [NL, 4, 2, 32, 64])
        self.s5d = d("s5d", [128, NL, 2])
        self.s5ds = d("s5ds", [64, NL])
        self.w_glu = d("w_glu", [NL, 256, 256])
        self.s5h0 = d("s5h0", [NL, 4, 128, 2])
        self.c_dk = d("c_dk", [NL, 256, 64])
        self.c_dv = d("c_dv", [NL, 256, 64])
        self.c_ckv = d("c_ckv", [NL, 256, 128])
        self.c_kr = d("c_kr", [NL, 256, 32])
        self.rope = d("rope", [2, 128, NS])
        self.identd = d("identd", [128, 128])
        o = self.dout
        self.y_tok = o("y_tok", [T, D])
        self.ndk = o("ndk", [2, NL, 256, 256])
        self.ndv = o("ndv", [2, NL, 256, 256])
        self.nckv = o("nckv", [2, NL, 256, 128])
        self.nkr = o("nkr", [2, NL, 256, 32])
        self.ns5 = o("ns5", [2, NL, 2, 16, 64, 2])
        self.x1_in = [self.dint("x1_in%d" % l, [4, SH, NS]) for l in range(NL)]
        self.x1_out = [self.dint("x1_out%d" % l, [4, 4 * SH, NS]) for l in range(NL)]
        self.g1_in = [self.dint("g1_in%d" % l, [160, NS]) for l in range(NL)]
        self.g1_out = [self.dint("g1_out%d" % l, [4 * 160, NS]) for l in range(NL)]
        self.own1 = [self.dint("own1_%d" % l, [4, SH, NS]) for l in range(NL)]
        self.own2 = [self.dint("own2_%d" % l, [4, 256, NS]) for l in range(NL)]
        self.e2_in = [self.dint("e2_in%d" % l, [256, 4096]) for l in range(NL)]
        self.e2_out = [self.dint("e2_out%d" % l, [2, 4 * 128, 4096]) for l in range(NL)]
        s = P.sbuf
        self.xT = s("xT", [128, 8, T], F32)
        self.ARA = s("ara", [128, 48160], BF16)
        self.ARB = s("arb", [128, 12288], BF16)
        self.B = [P.psum("B%d" % i, [128, 512], F32) for i in range(8)]
        self.ident_f = s("ident_f", [128, 128], F32)
        self.ident_b = s("ident_b", [128, 128], BF16)
        self.ones_f = s("ones_f", [128, 128], F32)
        self.ones_b = s("ones_b", [128, 128], BF16)
        self.eps_ln = s("eps_ln", [128, 1], F32)
        self.eps_rms = s("eps_rms", [128, 1], F32)
        self.MS = [s("MS%d" % l, [128, 48, 2], F32) for l in range(NL)]
        self.scb = s("scb", [128, 8, 2], BF16)
        self.cond_s = s("cond_s", [128, 8, 2], F32)
        self.bada_s = s("bada_s", [128, NL, 48], F32)
        self.lnp_s = s("lnp_s", [128, NL, 4, 8], F32)
        self.gq_s = s("gq_s", [128, NL, 2], F32)
        self.gkv_s = s("gkv_s", [128, NL], F32)
        self.dng_s = s("dng_s", [64, NL], F32)
        self.gsc = s("gsc", [64, NL], F32)
        self.lamt = s("lamt", [128, NL], F32)
        self.wuq = s("wuq", [128, 2, 768], BF16)
        self.wuk = s("wuk", [128, 512], BF16)
        self.wuko = s("wuko", [128, 128], BF16)
        self.wuv = s("wuv", [128, 512], BF16)
        self.wuvo = s("wuvo", [128, 128], BF16)
        self.wglu = s("wglu", [128, 2, 256], BF16)
        self.gkvb_s = s("gkvb_s", [128, 128], F32)
        self.F1 = s("F1", [128, 512], F32)
        self.F2 = s("F2", [128, 512], F32)
        self.F3 = s("F3", [128, 512], F32)
        self.F4 = s("F4", [128, 512], F32)
        self.STG = s("STG", [128, 4, 512], BF16)
        self.PT = s("PT", [128, 6, 512], BF16)
        self.QO = s("QO", [128, 3, 512], BF16)
        self.SQB = s("SQB", [128, 2, 512], BF16)
        self.small = s("small", [128, 64], F32)


A_HT = 0
A_ACT = 12288
A_WIN = 12288
A_WINR = 23808
A_WKR = 28160
A_WKRR = 28928
A_DQP = 29696
A_KDP = 30720
A_UTP = 31744
A_QNP = 32768
A_CKP = 33792
A_KRP = 34304
A_VDP = 34816
A_CAT = 35872
A_KDO = 0
A_VDO = 4352
A_CKB = 6596
A_KH = 12288
A_VH = 20992
A_KHP = 25480
A_VHP = 25992
A_XIN = 12288
A_WAB = 16384
A_WOUT = 0
A_YST = 8192
A_YP = 26256
A_YS = 7620
VW = 66
B_WGU = 0
B_WD = 6144
B_TAB = 0
B_CLHS = 4096
B_BLHS = 6144
B_BD = 6656
B_WORK = 7168
B_UCH = 10240
B_RR = 10752


def _stage0(self):
    P = self.P
    P.dma(self.ident_f[:], self.identd)
    P.copy(self.ident_b[:], self.ident_f[:])
    P.memset(self.ones_f[:], 1.0)
    P.memset(self.ones_b[:], 1.0)
    P.memset(self.eps_ln[:], LN_EPS / (ALPHA * ALPHA))
    P.memset(self.eps_rms[:], RMS_EPS)
    P.dma(self.cond_s[:], self.condT)
    P.dma(self.bada_s[:], self.badaT)
    P.dma(self.lnp_s[:], self.lnp)
    P.dma(self.gq_s[:], self.gq)
    P.dma(self.gkv_s[:], self.gkv)
    P.dma(self.dng_s[:], self.dng)
    lqk_s = self.F4[:, 0:256].rearrange('p (l n) -> p l n', n=128)
    P.dma(lqk_s, self.lqk)
    xin = self.av(A_XIN, [128, 2, D], F32)
    for tt in range(12):
        buf = xin[:, tt % 2, :]
        P.dma(buf, self.x_tok[tt * 128:(tt + 1) * 128, :])
        for half in range(2):
            bk = self.B[(tt * 2 + half) % 4]
            for kk in range(4):
                k = half * 4 + kk
                P.transpose(bk[:, kk * 128:(kk + 1) * 128], buf[:, k * 128:(k + 1) * 128], self.ident_f[:])
            P.copy(self.xT[:, half * 4:half * 4 + 4, tt * 128:(tt + 1) * 128],
                   bk[:].rearrange("p (k t) -> p k t", t=128), eng=self.cpeng())
    sm = self.small
    for l in range(NL):
        q = lqk_s[:, l, :]
        P.tt(sm[:, 0:32], q[:, 0:32], q[:, 32:64], ALU.mult)
        P.tt(sm[:, 32:64], q[:, 64:96], q[:, 96:128], ALU.mult)
        self.reduce_sum(self.F1[:, 0:1], sm[:, 0:32])
        self.reduce_sum(self.F1[:, 1:2], sm[:, 32:64])
        P.act(self.F1[:, 2:4], self.F1[:, 0:2], AF.Exp)
        P.tt(self.F1[:, 4:5], self.F1[:, 2:3], self.F1[:, 3:4], ALU.subtract)
        P.ts(self.lamt[:, l:l + 1], self.F1[:, 4:5], lam_init(l), None, ALU.add)
        P.ts(self.gsc[:, l:l + 1], self.dng_s[:, l:l + 1], 1.0 - lam_init(l), None, ALU.mult)
    P.act(self.scb[:], self.cond_s[:], AF.Silu)
    self.mods(0, part=0)


def _mods(self, l, part=None):
    P = self.P
    wab = self.av(A_WAB, [128, 2, 8, 512])
    c0, c1 = (0, 12) if part is None else ((0, 4) if part == 0 else (4, 12))
    for c in range(c0, c1):
        wa = wab[:, c % 2]
        P.dma(wa, self.w_ada[l, :, c * 512:(c + 1) * 512].rearrange("(k p) n -> p k n", p=128), eng=POOL)
        for j in range(4):
            jj = c * 4 + j
            for k in range(8):
                P.mm(self.B[7][:, jj * 2:jj * 2 + 2], wa[:, k, j * 128:(j + 1) * 128], self.scb[:, k, :],
                     start=(k == 0), stop=(k == 7))
    MS = self.MS[l]
    psv = self.B[7][:, 0:96].rearrange("p (j r) -> p j r", r=2)
    j0, j1 = 4 * c0, 4 * c1
    for r in range(2):
        P.tt(MS[:, j0:j1, r], psv[:, j0:j1, r], self.bada_s[:, l, j0:j1], ALU.add)
    if j0 <= 8 < j1:
        P.ts(MS[:, 8:16, :], MS[:, 8:16, :], 1.0, None, ALU.add)
    if j0 <= 32 < j1:
        P.ts(MS[:, 32:40, :], MS[:, 32:40, :], 1.0, None, ALU.add)
        P.ts(MS[:, 16:24, :], MS[:, 16:24, :], 1.0 / ALPHA, None, ALU.mult)
        P.ts(MS[:, 40:48, :], MS[:, 40:48, :], 1.0 / ALPHA, None, ALU.mult)


def _modulate(self, l, which):
    P = self.P
    hT = self.av(A_HT, [128, 8, T])
    MS = self.MS[l]
    jsh = 0 if which == 1 else 24
    jsc = 8 if which == 1 else 32
    for k in range(8):
        P.ts(hT[:, k, 0:NPR], self.xT[:, k, 0:NPR], MS[:, jsc + k, 0:1], MS[:, jsh + k, 0:1], ALU.mult, ALU.add)
        P.ts(hT[:, k, NPR:T], self.xT[:, k, NPR:T], MS[:, jsc + k, 1:2], MS[:, jsh + k, 1:2], ALU.mult, ALU.add)
    return hT


def _load_layer_weights(self, l):
    P = self.P
    win = self.av(A_WIN, [128, 8, 1440])
    winr = self.av(A_WINR, [128, 8, 544])
    wkr = self.av(A_WKR, [128, 8, 96])
    wkrr = self.av(A_WKRR, [128, 8, 96])
    self.ropet = self.av(A_CAT, [128, 2, NS])
    self.wuqr = self.av(A_CAT + 2048, [128, 2, 768])
    P.dma(self.ropet[:], self.rope.rearrange("a p n -> p a n"), eng=POOL)
    src = self.w_in[l].rearrange("(k p) n -> p k n", p=128)
    P.dma(win[:, 0:4, :], src[:, 0:4, :], eng=POOL)
    P.dma(win[:, 4:8, :], src[:, 4:8, :], eng=POOL)
    P.dma(self.wuq[:], self.w_uq[l].rearrange("(k p) n -> p k n", p=128), eng=POOL)
    P.dma(self.wuk[:], self.w_uk[l], eng=POOL)
    P.dma(self.wuv[:], self.w_uv[l], eng=POOL)
    P.dma(self.wuko[:], self.w_uk_own[l], eng=POOL)
    P.dma(self.wuvo[:], self.w_uv_own[l], eng=POOL)
    P.dma(self.wglu[:], self.w_glu[l].rearrange("(k p) n -> p k n", p=128), eng=POOL)
    P.dma(self.gkvb_s[:], self.gkv_b[l])
    for k in range(8):
        s4 = win[:, k, 0:512].rearrange("p (b h d) -> p b h d", h=2, d=16)
        d4 = winr[:, k, 0:512].rearrange("p (b h d) -> p b h d", h=2, d=16)
        P.ts(d4[:, :, 0, :], s4[:, :, 1, :], -1.0, None, ALU.mult)
        P.copy(d4[:, :, 1, :], s4[:, :, 0, :])
    P.memset(wkr[:], 0.0)
    P.memset(wkrr[:], 0.0)
    P.copy(wkr[:, :, 64:96], win[:, :, 1408:1440])
    P.ts(wkrr[:, :, 64:80], win[:, :, 1424:1440], -1.0, None, ALU.mult)
    P.copy(wkrr[:, :, 80:96], win[:, :, 1408:1424])
    for kt in range(2):
        P.ts(self.wuq[:, kt, :], self.wuq[:, kt, :], self.gq_s[:, l, kt:kt + 1], None, ALU.mult)
    P.memset(self.wuqr[:], 0.0)
    for kt in range(2):
        s3 = self.wuq[:, kt, :].rearrange("p (h c) -> p h c", c=96)
        d3 = self.wuqr[:, kt, :].rearrange("p (h c) -> p h c", c=96)
        P.ts(d3[:, :, 64:80], s3[:, :, 80:96], -1.0, None, ALU.mult)
        P.copy(d3[:, :, 80:96], s3[:, :, 64:80])
    return win, winr, wkr, wkrr


def _proj(self, lhs_fn, t0, n, nrows=128):
    P = self.P
    hT = self.av(A_HT, [128, 8, T])
    b = self.bank()
    for k in range(8):
        P.mm(b[0:nrows, 0:n], lhs_fn(k), hT[:, k, t0:t0 + n], start=(k == 0), stop=(k == 7))
    return b


def _rms_fm(self, pbanks, n, nfeat):
    P = self.P
    for i, pb in enumerate(pbanks):
        P.act(self.SQB[:, i, 0:n], pb[:, 0:n], AF.Square)
    ssb = self.B[6]
    for i in range(len(pbanks)):
        P.mm(ssb[:, 0:n], self.ones_b[:], self.SQB[:, i, 0:n], start=(i == 0), stop=(i == len(pbanks) - 1))
    P.act(self.F3[:, 0:n], ssb[:, 0:n], AF.Ln, bias=self.eps_rms[:, 0:1], scale=1.0 / nfeat)
    P.act(self.F4[:, 0:n], self.F3[:, 0:n], AF.Exp, scale=-0.5)
    return self.F4


def _phaseC(self, l, win, winr, wkr, wkrr):
    P = self.P
    hT = self.av(A_HT, [128, 8, T])
    x1 = self.x1_in[l]
    g1 = self.g1_in[l]
    cosT = self.ropet[:, 0, :]
    sinT = self.ropet[:, 1, :]
    n = NPR
    DQP = self.av(A_DQP, [128, 2, 512])
    KDP = self.av(A_KDP, [128, 2, 512])
    UTP = self.av(A_UTP, [128, 2, 512])
    QNP = self.av(A_QNP, [128, 2, 512])
    CKP = self.av(A_CKP, [128, 512])
    KRP = self.av(A_KRP, [128, 512])
    VDP = self.av(A_VDP, [128, 4, 4, VW])
    P.memset(VDP[:], 1.0)
    for j in range(2):
        b = self.proj(lambda k, j=j: win[:, k, j * 128:(j + 1) * 128], 0, n)
        P.copy(DQP[:, j, :], b[:, 0:n], eng=self.cpeng())
        b = self.proj(lambda k, j=j: win[:, k, 256 + j * 128:256 + (j + 1) * 128], 0, n)
        P.copy(KDP[:, j, :], b[:, 0:n], eng=self.cpeng())
        b = self.proj(lambda k, j=j: win[:, k, 768 + j * 128:768 + (j + 1) * 128], 0, n)
        P.copy(UTP[:, j, :], b[:, 0:n], eng=self.cpeng())
    bq = [self.proj(lambda k, j=j: win[:, k, 1024 + j * 128:1024 + (j + 1) * 128], 0, n) for j in range(2)]
    rstd = self.rms_fm(bq, n, 256)
    for j in range(2):
        P.tt(QNP[:, j, :], bq[j][:, 0:n], rstd[:, 0:n], ALU.mult)
    bk = self.proj(lambda k: win[:, k, 1280:1408], 0, n)
    rstd = self.rms_fm([bk], n, 128)
    P.stt(CKP[:, :], bk[:, 0:n], self.gkv_s[:, l:l + 1], rstd[:, 0:n], ALU.mult, ALU.mult)
    if self.stop_after == "c1":
        return
    b = self.proj(lambda k: wkr[:, k, :], 0, n, nrows=96)
    P.copy(KRP[64:96, :], b[64:96, 0:n])
    if self.stop_after == "c2":
        return
    for tt in range(4):
        seq, t0 = tt // 2, (tt % 2) * 128
        b1 = self.bank()
        b2 = self.bank()
        for k in range(8):
            P.mm(b1[:, 0:512], hT[:, k, tt * 128:(tt + 1) * 128], win[:, k, 256:768], start=(k == 0), stop=(k == 7))
        for k in range(8):
            P.mm(b2[:, 0:160], hT[:, k, tt * 128:(tt + 1) * 128], win[:, k, 1280:1440], start=(k == 0), stop=(k == 7))
        P.copy(self.F1[:, 0:512], b1[:, 0:512], eng=ACT)
        P.dma(self.ndk[seq, l, t0:t0 + 128, :], self.F1[:, 0:256])
        P.dma(self.ndv[seq, l, t0:t0 + 128, :], self.F1[:, 256:512])
        if self.stop_after == "d1":
            continue
        P.copy(VDP[:, tt, :, 0:64], b1[:, 256:512].rearrange("p (h c) -> p h c", c=64))
        if self.stop_after == "d2":
            continue
        P.act(self.F2[:, 0:128], b2[:, 0:128], AF.Square, accum_out=self.small[:, 8:9])
        P.act(self.small[:, 9:10], self.small[:, 8:9], AF.Ln, bias=self.eps_rms[:, 0:1], scale=1.0 / 128)
        P.act(self.small[:, 10:11], self.small[:, 9:10], AF.Exp, scale=-0.5)
        P.stt(self.F2[:, 128:256], b2[:, 0:128], self.small[:, 10:11], self.gkvb_s[:, :], ALU.mult, ALU.mult)
        P.dma(self.nckv[seq, l, t0:t0 + 128, :], self.F2[:, 128:256])
        if self.stop_after == "d3":
            continue
        P.copy(self.F2[:, 256:288], b2[:, 128:160])
        P.dma(self.nkr[seq, l, t0:t0 + 128, :], self.F2[:, 256:288])
    if self.stop_after in ("c3", "d1", "d2", "d3"):
        return
    for sb in range(2):
        t0 = NPR + sb * 512
        s0 = sb * 512
        n = 512
        cb = cosT[:, s0:s0 + n]
        snb = sinT[:, s0:s0 + n]

        def roped(ba, bb, rows=slice(0, 128)):
            st = self.stgbuf()
            P.tt(self.F1[rows, 0:n], ba[rows, 0:n], cb[rows], ALU.mult)
            P.tt(self.F2[rows, 0:n], bb[rows, 0:n], snb[rows], ALU.mult)
            P.tt(st[rows, 0:n], self.F1[rows, 0:n], self.F2[rows, 0:n], ALU.add)
            return st
        for j in range(2):
            ba = self.proj(lambda k, j=j: win[:, k, j * 128:(j + 1) * 128], t0, n)
            bb = self.proj(lambda k, j=j: winr[:, k, j * 128:(j + 1) * 128], t0, n)
            st = roped(ba, bb)
            for i in range(2):
                h = 2 * j + i
                P.dma(x1[h, 0:64, s0:s0 + n], st[i * 64:(i + 1) * 64, 0:n])
            ba = self.proj(lambda k, j=j: win[:, k, 256 + j * 128:256 + (j + 1) * 128], t0, n)
            bb = self.proj(lambda k, j=j: winr[:, k, 256 + j * 128:256 + (j + 1) * 128], t0, n)
            st = roped(ba, bb)
            for i in range(2):
                h = 2 * j + i
                P.dma(x1[h, 64:128, s0:s0 + n], st[i * 64:(i + 1) * 64, 0:n])
            ba = self.proj(lambda k, j=j: win[:, k, 768 + j * 128:768 + (j + 1) * 128], t0, n)
            st = self.stgbuf()
            P.copy(st[:, 0:n], ba[:, 0:n], eng=self.cpeng())
            for i in range(2):
                h = 2 * j + i
                P.dma(x1[h, 128:192, s0:s0 + n], st[i * 64:(i + 1) * 64, 0:n])
        if self.stop_after == "c4":
            return
        for tt in range(4):
            b1 = self.bank()
            for k in range(8):
                P.mm(b1[:, 0:256], hT[:, k, t0 + tt * 128:t0 + (tt + 1) * 128], win[:, k, 512:768],
                     start=(k == 0), stop=(k == 7))
            st = self.stgbuf()
            P.copy(st[:, 0:256], b1[:, 0:256], eng=self.cpeng())
            for h in range(4):
                if self.stop_after == "e1":
                    continue
                dst = x1[h, 192:256, :].rearrange("r (a b) -> (r a) b", b=64)
                P.dma(dst[s0 + tt * 128:s0 + (tt + 1) * 128, :], st[:, h * 64:(h + 1) * 64])
        if self.stop_after in ("e1", "e2"):
            return
        if self.stop_after == "c5":
            return
        bq = [self.proj(lambda k, j=j: win[:, k, 1024 + j * 128:1024 + (j + 1) * 128], t0, n) for j in range(2)]
        rstd = self.rms_fm(bq, n, 256)
        qn = self.QO
        for j in range(2):
            P.tt(qn[:, j, 0:n], bq[j][:, 0:n], rstd[:, 0:n], ALU.mult)
        for hq in range(8):
            ba = self.bank()
            bb = self.bank()
            for kt in range(2):
                P.mm(ba[0:96, 0:n], self.wuq[:, kt, hq * 96:(hq + 1) * 96], qn[:, kt, 0:n], start=(kt == 0), stop=(kt == 1))
            for kt in range(2):
                P.mm(bb[0:96, 0:n], self.wuqr[:, kt, hq * 96:(hq + 1) * 96], qn[:, kt, 0:n], start=(kt == 0), stop=(kt == 1))
            st = roped(ba, bb, rows=slice(64, 96))
            P.copy(st[0:64, 0:n], ba[0:64, 0:n], eng=self.cpeng())
            sh, w = hq // 2, hq % 2
            P.dma(x1[sh, 256 + w * 96:256 + (w + 1) * 96, s0:s0 + n], st[0:96, 0:n])
        bk = self.proj(lambda k: win[:, k, 1280:1408], t0, n)
        rstd = self.rms_fm([bk], n, 128)
        st = self.stgbuf()
        P.stt(st[:, 0:n], bk[:, 0:n], self.gkv_s[:, l:l + 1], rstd[:, 0:n], ALU.mult, ALU.mult)
        P.dma(g1[0:128, s0:s0 + n], st[:, 0:n])
        ba = self.proj(lambda k: wkr[:, k, :], t0, n, nrows=96)
        bb = self.proj(lambda k: wkrr[:, k, :], t0, n, nrows=96)
        st = roped(ba, bb, rows=slice(64, 96))
        P.dma(g1[128:160, s0:s0 + n], st[64:96, 0:n])


KB.stage0 = _stage0
KB.mods = _mods
KB.modulate = _modulate
KB.load_layer_weights = _load_layer_weights
KB.proj = _proj
KB.rms_fm = _rms_fm
KB.phaseC = _phaseC


def _step_side(self):
    side = getattr(self, "side", None)
    if side is not None:
        try:
            next(side)
        except StopIteration:
            self.side = None


def _attn_pass(self, specs, nkt, nq, scale):
    P = self.P
    LAG = 4 if len(specs) == 1 else 2
    for sp in specs:
        sp["sb"] = {}
        sp["pt"] = {}
    for step in range(nkt + LAG):
        if step < nkt:
            for sp in specs:
                sb = self.sbank()[:, 0:nq]
                sp["sb"][step] = sb
                kw = {}
                if sp.get("tp") is not None:
                    kw["tile_position"] = sp["tp"]
                P.mm(sb, sp["K"](step), sp["Q"], start=True, stop=True, **kw)
            for sp in specs:
                pt = self.ptbuf()[:, 0:nq]
                P.act(pt, sp["sb"][step], AF.Exp, scale=scale)
                sp["pt"][step] = pt
        if step >= LAG:
            kt = step - LAG
            for sp in specs:
                P.mm(sp["O"], sp["V"](kt), sp["pt"][kt], start=(kt == 0), stop=(kt == nkt - 1))
        if step % 16 == 15:
            self.step_side()


def _fin_mla(self, O, nq):
    P = self.P
    F1, F2 = self.F1, self.F2
    P.copy(F1[0:65, 0:nq], O[0:65, 0:nq])
    P.recip(F2[64:65, 0:nq], F1[64:65, 0:nq])
    bc = self.sbank()[0:64, 0:nq]
    P.mm(bc, self.ones_f[64:65, 0:64], F2[64:65, 0:nq])
    st = self.stgbuf()
    P.tt(st[0:64, 0:nq], F1[0:64, 0:nq], bc, ALU.mult)
    return st


def _fin_diff(self, O1, O2, nq, l):
    P = self.P
    F1, F2, F3, F4 = self.F1, self.F2, self.F3, self.F4
    P.copy(F1[0:65, 0:nq], O1[0:65, 0:nq])
    P.copy(F2[0:65, 0:nq], O2[0:65, 0:nq], eng=ACT)
    P.recip(F3[64:65, 0:nq], F1[64:65, 0:nq])
    P.recip(F4[64:65, 0:nq], F2[64:65, 0:nq])
    P.ts(F4[64:65, 0:nq], F4[64:65, 0:nq], self.lamt[64:65, l:l + 1], None, ALU.mult)
    bc = self.sbank()[0:64, 0:nq]
    P.mm(bc, self.ones_f[64:65, 0:64], F3[64:65, 0:nq])
    P.tt(F1[0:64, 0:nq], F1[0:64, 0:nq], bc, ALU.mult)
    bc = self.sbank()[0:64, 0:nq]
    P.mm(bc, self.ones_f[64:65, 0:64], F4[64:65, 0:nq])
    P.tt(F2[0:64, 0:nq], F2[0:64, 0:nq], bc, ALU.mult)
    P.tt(F1[0:64, 0:nq], F1[0:64, 0:nq], F2[0:64, 0:nq], ALU.subtract)
    P.act(self.SQB[0:64, 0, 0:nq], F1[0:64, 0:nq], AF.Square)
    bc = self.sbank()[0:64, 0:nq]
    P.mm(bc, self.ones_b[0:64, 0:64], self.SQB[0:64, 0, 0:nq])
    P.act(F3[0:64, 0:nq], bc, AF.Ln, bias=self.eps_rms[0:64, 0:1], scale=1.0 / 64)
    P.act(F4[0:64, 0:nq], F3[0:64, 0:nq], AF.Exp, scale=-0.5)
    st = self.stgbuf()
    P.stt(st[0:64, 0:nq], F1[0:64, 0:nq], self.gsc[:, l:l + 1], F4[0:64, 0:nq], ALU.mult, ALU.mult)
    return st


def _prompt_attention(self, l):
    P = self.P
    DQP = self.av(A_DQP, [128, 2, 512])
    KDP = self.av(A_KDP, [128, 2, 512])
    QNP = self.av(A_QNP, [128, 2, 512])
    CKP = self.av(A_CKP, [128, 512])
    KRP = self.av(A_KRP, [128, 512])
    VDP = self.av(A_VDP, [128, 4, 4, VW])
    KHP = self.av(A_KHP, [128, 2, 256])
    VHP = self.av(A_VHP, [128, 2, 2, VW])
    CAT = self.av(A_CAT, [128, 8, T])
    P.memset(VHP[:], 1.0)
    nq = 256
    cnt = 0
    for seq in range(2):
        cols = slice(seq * 256, (seq + 1) * 256)
        for h in range(4):
            j, rb = h // 2, (h % 2) * 64
            specs = []
            for c in range(2):
                r0 = rb + 32 * c
                specs.append(dict(
                    K=lambda kt, r0=r0, j=j, seq=seq: KDP[r0:r0 + 32, j, seq * 256 + kt * 128:seq * 256 + (kt + 1) * 128],
                    Q=DQP[r0:r0 + 32, j, cols],
                    V=lambda kt, h=h, seq=seq: VDP[:, seq * 2 + kt, h, 0:65],
                    O=self.B[4 + c][0:65, 0:nq], tp=(r0, 0)))
            self.attn_pass(specs, 2, nq, 32 ** -0.5)
            st = self.fin_diff(specs[0]["O"], specs[1]["O"], nq, l)
            P.dma(CAT[rb:rb + 64, j, cols], st[0:64, 0:nq])
            self.step_side()
        for h in range(8):
            buf = cnt % 2
            cnt += 1
            ps = self.bank()
            P.mm(ps[0:64, 0:nq], self.wuk[:, h * 64:(h + 1) * 64], CKP[:, cols])
            P.copy(KHP[0:64, buf, :], ps[0:64, 0:nq])
            P.copy(KHP[64:96, buf, :], KRP[64:96, cols])
            ps2 = self.bank()
            for kt in range(2):
                P.mm(ps2[:, kt * 64:(kt + 1) * 64], CKP[:, seq * 256 + kt * 128:seq * 256 + (kt + 1) * 128],
                     self.wuv[:, h * 64:(h + 1) * 64])
            P.copy(VHP[:, buf, :, 0:64], ps2[:, 0:128].rearrange("p (t c) -> p t c", c=64), eng=ACT)
            ps3 = self.bank()
            for kt in range(2):
                P.mm(ps3[0:96, 0:nq], self.wuq[:, kt, h * 96:(h + 1) * 96], QNP[:, kt, cols], start=(kt == 0), stop=(kt == 1))
            P.copy(self.QO[0:96, buf, 0:nq], ps3[0:96, 0:nq])
            O = self.B[4 + buf][0:65, 0:nq]
            specs = [dict(K=lambda kt, buf=buf: KHP[0:96, buf, kt * 128:(kt + 1) * 128],
                          Q=self.QO[0:96, buf, 0:nq],
                          V=lambda kt, buf=buf: VHP[:, buf, kt, 0:65], O=O, tp=None)]
            self.attn_pass(specs, 2, nq, 96 ** -0.5)
            st = self.fin_mla(O, nq)
            P.dma(CAT[(h % 2) * 64:(h % 2) * 64 + 64, 4 + h // 2, cols], st[0:64, 0:nq])
            self.step_side()


def _dyn_dma(self, out, src_fn, reads, writes, kind=0):
    def fn(e, out=out, src_fn=src_fn, kind=kind):
        if getattr(self, "_rank_val", None) is None:
            rk = e.partition_id() % 4
            self._rank_val = (e.snap(rk), e.snap(rk * NS))
        r = self._rank_val[kind]
        src = src_fn(r)
        try:
            return e.dma_start(out=out, in_=src)
        except Exception:
            raise
    op = Op(POOL, fn, "dyn")
    return self.P.dma_custom(op, reads=reads, writes=writes)


def _sample_loads(self, l):
    P = self.P
    eo = self.x1_out[l]
    xo4 = eo.rearrange("s (j w) t -> s j w t", j=4)
    g3 = self.g1_out[l].rearrange("(j w) t -> j w t", j=4)
    KDO = self.av(A_KDO, [128, NK])
    VDO = self.av(A_VDO, [128, NKT, VW])
    KH = self.av(A_KH, [128, 2, NK])
    VH = self.av(A_VH, [128, 2, NKT, VW])
    P.memset(VDO[:, :, 64:66], 1.0)
    P.memset(VH[:, :, :, 64:66], 1.0)
    o1 = self.own1[l]
    self.dyn_dma(o1, lambda r: xo4[ds(r, 1)].rearrange('o j w t -> (o j) w t'), reads=[eo], writes=[o1])
    P.dma(KDO[0:64, 0:4096].rearrange("w (j t) -> w j t", j=4), o1[:, 64:128, :].rearrange("j w t -> w j t"))
    for j in range(4):
        P.dma(VDO[:, j * 8:(j + 1) * 8, 0:64],
              o1[j, 192:256, :].rearrange("w (a b) -> (w a) b", b=64).rearrange("(t p) c -> p t c", p=128))
    for hh in range(2):
        P.dma(KH[64:96, hh, 0:4096].rearrange("w (j t) -> w j t", j=4),
              g3[:, 128:160, :].rearrange("j w t -> w j t"))
    ctmp = self.F1[:, 0:512].rearrange("p (i c) -> p i c", i=2)
    P.dma(ctmp[:, :, 0:64], self.c_dk[l].rearrange("(i p) c -> p i c", p=128))
    for i in range(2):
        ps = self.bank()
        P.transpose(ps[0:64, 0:128], ctmp[:, i, 0:64], self.ident_f[:])
        P.copy(KDO[0:64, 4096 + i * 128:4096 + (i + 1) * 128], ps[0:64, 0:128])
    P.dma(VDO[:, 32:34, 0:64], self.c_dv[l].rearrange("(i p) c -> p i c", p=128), eng=POOL)
    ktmp = self.F2[:, 0:192].rearrange("p (i c) -> p i c", i=2)
    P.memset(ktmp, 0.0)
    P.dma(ktmp[:, :, 64:96], self.c_kr[l].rearrange("(i p) c -> p i c", p=128))
    for i in range(2):
        ps = self.bank()
        P.transpose(ps[0:96, 0:128], ktmp[:, i, :], self.ident_f[:])
        for hh in range(2):
            P.copy(KH[64:96, hh, 4096 + i * 128:4096 + (i + 1) * 128], ps[64:96, 0:128])
    CKB = self.av(A_CKB, [128, 2, 512])
    ctmp2 = self.F3[:, 0:256].rearrange("p (i c) -> p i c", i=2)
    for kb in range(9):
        n = 512 if kb < 8 else 256
        cb = CKB[:, kb % 2, 0:n]
        if kb < 8:
            j, c0 = kb // 2, (kb % 2) * 512
            P.dma(cb, g3[j, 0:128, c0:c0 + 512])
        else:
            P.dma(ctmp2, self.c_ckv[l].rearrange("(i p) c -> p i c", p=128))
            for i in range(2):
                ps = self.bank()
                P.transpose(ps[:, 0:128], ctmp2[:, i, :], self.ident_f[:])
                P.copy(cb[:, i * 128:(i + 1) * 128], ps[:, 0:128])
        nt = n // 128
        psv = self.bank()
        for hh in range(2):
            ps = self.bank()
            P.mm(ps[0:64, 0:n], self.wuko[:, hh * 64:(hh + 1) * 64], cb)
            P.copy(KH[0:64, hh, kb * 512:kb * 512 + n], ps[0:64, 0:n], eng=self.cpeng())
        for t in range(nt):
            for hh in range(2):
                P.mm(psv[:, (t * 2 + hh) * 64:(t * 2 + hh + 1) * 64], cb[:, t * 128:(t + 1) * 128],
                     self.wuvo[:, hh * 64:(hh + 1) * 64])
        pv4 = psv[:, 0:nt * 128].rearrange("p (t h c) -> p t h c", h=2, c=64)
        for hh in range(2):
            P.copy(VH[:, hh, kb * 4:kb * 4 + nt, 0:64], pv4[:, :, hh, :], eng=self.cpeng())


def _sample_attention(self, l, side=None):
    P = self.P
    KDO = self.av(A_KDO, [128, NK])
    VDO = self.av(A_VDO, [128, NKT, VW])
    KH = self.av(A_KH, [128, 2, NK])
    VH = self.av(A_VH, [128, 2, NKT, VW])
    e2 = self.e2_in[l]
    QO = self.QO
    nq = 512

    step_side = self.step_side
    for qb in range(8):
        j, c0 = qb // 2, (qb % 2) * 512
        o1 = self.own1[l]
        P.dma(QO[0:64, 0, :], o1[j, 0:64, c0:c0 + 512])
        for hh in range(2):
            P.dma(QO[0:96, 1 + hh, :], o1[j, 256 + hh * 96:256 + (hh + 1) * 96, c0:c0 + 512])
        specs = []
        for c in range(2):
            r0 = 32 * c
            specs.append(dict(K=lambda kt, r0=r0: KDO[r0:r0 + 32, kt * 128:(kt + 1) * 128],
                              Q=QO[r0:r0 + 32, 0, :],
                              V=lambda kt: VDO[:, kt, 0:65],
                              O=self.B[4 + c][0:65, 0:nq], tp=(r0, 0)))
        self.attn_pass(specs, NKT, nq, 32 ** -0.5)
        st = self.fin_diff(specs[0]["O"], specs[1]["O"], nq, l)
        P.dma(e2[0:64, qb * 512:(qb + 1) * 512], st[0:64, 0:nq])
        step_side()
        for hh in range(2):
            O = self.B[4 + hh][0:65, 0:nq]
            specs = [dict(K=lambda kt, hh=hh: KH[0:96, hh, kt * 128:(kt + 1) * 128],
                          Q=QO[0:96, 1 + hh, :],
                          V=lambda kt, hh=hh: VH[:, hh, kt, 0:65], O=O, tp=None)]
            self.attn_pass(specs, NKT, nq, 96 ** -0.5)
            st = self.fin_mla(O, nq)
            P.dma(e2[64 + hh * 64:128 + hh * 64, qb * 512:(qb + 1) * 512], st[0:64, 0:nq])
            step_side()
    while self.side is not None:
        self.step_side()


KB.attn_pass = _attn_pass
KB.step_side = _step_side
KB.fin_mla = _fin_mla
KB.fin_diff = _fin_diff
KB.prompt_attention = _prompt_attention
KB.dyn_dma = _dyn_dma
KB.sample_loads = _sample_loads
KB.sample_attention = _sample_attention


TWO_PI = 2.0 * math.pi


def _rr(self, out, ang, shape):
    P = self.P
    n = shape[1]
    KI = self.bv(B_RR, [128, 256], F32).bitcast(I32)[0:shape[0], 0:n]
    P.ts(KI, ang, 1.0 / TWO_PI, None, ALU.mult)
    P.stt(out, KI, -TWO_PI, ang, ALU.mult, ALU.add)
    P.ts(out, out, -3.141592, 3.141592, ALU.max, ALU.min)


def _s5_prep(self, l, prm_src, b_src, c_src, nt, tile_ids, qs, mcols, h0_src=None):
    P = self.P
    sp = self.s5sm
    PR = self.s5prm
    for i, s in enumerate(tile_ids):
        pass
    for i, s in enumerate(tile_ids):
        P.dma(PR[:, :, i:i + 1], prm_src[:, :, s:s + 1], allow_slow_non_contiguous=True)
    lre, lim, ldt = PR[:, 0, 0:nt], PR[:, 1, 0:nt], PR[:, 2, 0:nt]
    f = lambda k: sp[:, k, 0:nt]
    DT, TH, MAG, S1, C1, ARE, AIM, DEN, NRE, FRE, FIM, T0, T1, CT, ST, NST = [f(k) for k in range(16)]
    P.act(DT, ldt, AF.Exp)
    P.tt(TH, lim, DT, ALU.mult)
    P.tt(T0, lre, DT, ALU.mult)
    P.act(MAG, T0, AF.Exp)
    self.rr(T0, TH, [128, nt])
    P.act(S1, T0, AF.Sin)
    P.ts(T1, TH, math.pi / 2, None, ALU.add)
    self.rr(T0, T1, [128, nt])
    P.act(C1, T0, AF.Sin)
    P.tt(ARE, MAG, C1, ALU.mult)
    P.tt(AIM, MAG, S1, ALU.mult)
    P.tt(DEN, lre, lre, ALU.mult)
    P.tt(T0, lim, lim, ALU.mult)
    P.tt(DEN, DEN, T0, ALU.add)
    P.recip(DEN, DEN)
    P.ts(NRE, ARE, -1.0, None, ALU.add)
    P.tt(T0, NRE, lre, ALU.mult)
    P.tt(T1, AIM, lim, ALU.mult)
    P.tt(T0, T0, T1, ALU.add)
    P.tt(FRE, T0, DEN, ALU.mult)
    P.tt(T0, AIM, lre, ALU.mult)
    P.tt(T1, NRE, lim, ALU.mult)
    P.tt(T0, T0, T1, ALU.subtract)
    P.tt(FIM, T0, DEN, ALU.mult)
    P.ts(T1, TH, 256.0, None, ALU.mult)
    self.rr(T0, T1, [128, nt])
    P.act(ST, T0, AF.Sin)
    P.ts(T1, T1, math.pi / 2, None, ALU.add)
    self.rr(T0, T1, [128, nt])
    P.act(CT, T0, AF.Sin)
    P.ts(NST, ST, -1.0, None, ALU.mult)
    P.ts(T1, FIM, -1.0, None, ALU.mult)
    NFIM = T1
    TAB = self.bv(B_TAB, [128, 8, 2, 256])
    CL = self.bv(B_CLHS, [128, 8, 2, 128])
    BL = self.bv(B_BLHS, [128, 2, 2, 128])
    BD = self.bv(B_BD, [128, 2, 128], F32)
    ANG = self.bv(B_RR + 512, [128, 256], F32)
    RED = self.bv(B_RR + 1024, [128, 256], F32)
    P.memset(CL[:], 0.0)
    for i, s in enumerate(tile_ids):
        q = qs[i]
        d = i // (nt // 2)
        P.ts(ANG, self.iota_f[:, 0:256], TH[:, i:i + 1], None, ALU.mult)
        self.rr(RED, ANG, [128, 256])
        P.act(TAB[:, i, 1, :], RED, AF.Sin)
        P.ts(ANG, ANG, math.pi / 2, None, ALU.add)
        self.rr(RED, ANG, [128, 256])
        P.act(TAB[:, i, 0, :], RED, AF.Sin)
        bt = self.s5bt
        P.dma(bt[:], b_src[s].rearrange("c p h -> p c h"))
        P.memset(BD[:], 0.0)
        for g2 in range(2):
            rows = slice(64 * g2, 64 * g2 + 64)
            cs = slice(32 * q + 16 * g2, 32 * q + 16 * g2 + 16)
            P.ts(self.s5t16[rows, :], bt[rows, 0, :], FRE[rows, i:i + 1], None, ALU.mult)
            P.stt(BD[rows, 0, cs], bt[rows, 1, :], NFIM[rows, i:i + 1], self.s5t16[rows, :], ALU.mult, ALU.add)
            P.ts(self.s5t16[rows, :], bt[rows, 1, :], FRE[rows, i:i + 1], None, ALU.mult)
            P.stt(BD[rows, 1, cs], bt[rows, 0, :], FIM[rows, i:i + 1], self.s5t16[rows, :], ALU.mult, ALU.add)
        for c in range(2):
            ps = self.B[6]
            P.transpose(ps[:, 0:128], BD[:, c, :], self.ident_f[:])
            P.copy(BL[32 * q:32 * q + 32, d, c, :], ps[32 * q:32 * q + 32, 0:128]) if False else None
            P.copy(self.blall[32 * q:32 * q + 32, i, c, :], ps[32 * q:32 * q + 32, 0:128])
        cd = self.s5cd
        P.memset(cd[:], 0.0)
        P.dma(cd[0:16, :, 0:64], c_src[s][:, 0:16, :].rearrange("c n p -> n c p"))
        P.dma(cd[16:32, :, 64:128], c_src[s][:, 16:32, :].rearrange("c n p -> n c p"))
        for c in range(2):
            ps = self.B[6]
            P.transpose(ps[:, 0:32], cd[:, c, :], self.ident_f[0:32, 0:32])
            if c == 0:
                P.copy(CL[:, i, c, 32 * q:32 * q + 32], ps[:, 0:32])
            else:
                P.ts(CL[:, i, c, 32 * q:32 * q + 32], ps[:, 0:32], -1.0, None, ALU.mult)
    return dict(MAG=MAG, CT=CT, ST=ST, NST=NST, TAB=TAB, CL=CL)


def _s5_flush(self):
    pend = getattr(self, "s5_pending", None)
    if pend:
        for f in pend:
            f()
    self.s5_pending = []


def _s5_bmm(self, i, q, u_rows):
    P = self.P
    self._bb_rr = getattr(self, "_bb_rr", 0) + 1
    bb = self.B[3] if self._bb_rr % 2 else self.B[6]
    P.mm(bb[:, 0:256], self.blall[32 * q:32 * q + 32, i, 0, :], u_rows, tile_position=(32 * q, 0))
    P.mm(bb[:, 256:512], self.blall[32 * q:32 * q + 32, i, 1, :], u_rows, tile_position=(32 * q, 0))
    return bb


def _s5_chunk(self, pr, i, q, u_rows, init, rev, ybank, ycols, first, last, mcols, bbank=None):
    P = self.P
    W = lambda k, dt=BF16: self.bv(B_WORK + k * 256, [128, 256], dt)
    XR, XI, HR, HI, T1, T2, T3, T4 = [W(k) for k in range(8)]
    self._h_rr = getattr(self, "_h_rr", 0) + 1
    if self._h_rr % 2:
        HR = self.av(27280, [128, 256])
        HI = self.av(27280 + 256, [128, 256])
    GR = self.bv(B_WORK + 2048, [128, 256], F32)
    GI = self.bv(B_WORK + 2560, [128, 256], F32)
    TAB = pr["TAB"]
    cos = TAB[:, i, 0, :]
    sin = TAB[:, i, 1, :]
    if rev:
        cos = cos[:, ::-1]
        sin = sin[:, ::-1]
    bb = self.s5_bmm(i, q, u_rows) if bbank is None else bbank
    bre, bim = bb[:, 0:256], bb[:, 256:512]
    P.tt(T1, bre, cos, ALU.mult)
    P.tt(T2, bim, sin, ALU.mult)
    P.tt(XR, T1, T2, ALU.add)
    P.tt(T3, bim, cos, ALU.mult)
    P.tt(T4, bre, sin, ALU.mult)
    P.tt(XI, T3, T4, ALU.subtract)
    magbc = pr["MAG"][:, i:i + 1].to_broadcast([128, 256])
    ir = init[0] if init is not None else 0.0
    ii = init[1] if init is not None else 0.0
    sl = (lambda a: a[:, ::-1]) if rev else (lambda a: a)
    P.scan(sl(GR), magbc, sl(XR), ir)
    P.scan(sl(GI), magbc, sl(XI), ii)
    PE_ = self.post_eng
    P.tt(T1, GR, cos, ALU.mult, eng=PE_)
    P.tt(T2, GI, sin, ALU.mult, eng=PE_)
    P.tt(HR, T1, T2, ALU.subtract, eng=PE_)
    P.tt(T3, GR, sin, ALU.mult, eng=PE_)
    P.tt(T4, GI, cos, ALU.mult, eng=PE_)
    P.tt(HI, T3, T4, ALU.add, eng=PE_)
    e = 0 if rev else 255
    st = self.s5st
    tmp = self.s5t16
    P.ts(tmp[:, 0:1], GR[:, e:e + 1], pr["CT"][:, i:i + 1], None, ALU.mult)
    P.ts(tmp[:, 1:2], GR[:, e:e + 1], pr["ST"][:, i:i + 1], None, ALU.mult)
    P.stt(st[:, i, 0:1], GI[:, e:e + 1], pr["NST"][:, i:i + 1], tmp[:, 0:1], ALU.mult, ALU.add)
    P.stt(st[:, i, 1:2], GI[:, e:e + 1], pr["CT"][:, i:i + 1], tmp[:, 1:2], ALU.mult, ALU.add)
    CL = pr["CL"]

    def cmm(HR=HR, HI=HI, i=i, first=first, last=last):
        P.mm(ybank[0:mcols, ycols], CL[:, i, 0, 0:mcols], HR, start=first, stop=False)
        P.mm(ybank[0:mcols, ycols], CL[:, i, 1, 0:mcols], HI, start=False, stop=last)
    self.s5_flush()
    self.s5_pending = [cmm]


def _s5_prompt(self, l):
    P = self.P
    UTP = self.av(A_UTP, [128, 2, 512])
    YP = self.av(A_YP, [128, 2, 512])
    for j in range(2):
        tids = [d * 8 + 4 * j + q for d in range(2) for q in range(4)]
        qs = [q for d in range(2) for q in range(4)]
        pr = self.s5_prep(l, self.s5p[l], self.s5b[l], self.s5c[l], 8, tids, qs, 128)
        yield
        P.ts(self.diagd[:], self.ident_b[:], self.s5d_s[:, l, j:j + 1], None, ALU.mult)
        for seq in range(2):
            cols = slice(seq * 256, (seq + 1) * 256)
            yb = self.B[7]
            self.s5_flush()
            P.mm(yb[:, 0:256], self.diagd[:], UTP[:, j, cols], start=True, stop=False)
            nxt = self.s5_bmm(0, 0, UTP[0:32, j, cols]) if self.prefetch else None
            for i in range(8):
                d, q = i // 4, i % 4
                cur = nxt
                if not self.prefetch:
                    cur = None
                elif i + 1 < 8:
                    q2 = (i + 1) % 4
                    nxt = self.s5_bmm(i + 1, q2, UTP[32 * q2:32 * q2 + 32, j, cols])
                self.s5_chunk(pr, i, q, UTP[32 * q:32 * q + 32, j, cols], None, d == 1, yb, slice(0, 256),
                              False, i == 7, 128, bbank=cur)
                gp = 4 * j + q
                P.dma(self.ns5[seq, l, d, 2 * gp:2 * gp + 2].rearrange("g p c -> (g p) c"), self.s5st[:, i, :])
                yield
            self.s5_flush()
            P.copy(YP[:, j, cols], yb[:, 0:256])


def _s5_sample(self, l):
    P = self.P
    tids = [0, 1, 2, 3]
    qs = [0, 1, 0, 1]
    pr = self.s5_prep(l, self.s5ps[l], self.s5bs[l], self.s5cs[l], 4, tids, qs, 64)
    UCH = self.bv(B_UCH, [128, 2, 256])
    YS = self.av(A_YS, [128, 4096])
    st = self.s5st
    for i in range(4):
        P.dma(st[:, i, :], self.s5h0[l, i])
    P.ts(self.diagd[0:64, 0:64], self.ident_b[0:64, 0:64], self.s5ds_s[:, l:l + 1], None, ALU.mult)
    yield
    cnt = 0
    for d in range(2):
        order = range(16) if d == 0 else range(15, -1, -1)
        for c in order:
            j, c0 = c // 4, (c % 4) * 256
            ub = UCH[0:64, cnt % 2, :]
            cnt += 1
            P.dma(ub, self.own1[l][j, 128:192, c0:c0 + 256])
            yb = self.B[7]
            self.s5_flush()
            if d == 1:
                P.mm(yb[0:64, 0:256], self.diagd[0:64, 0:64], ub, start=True, stop=False)
            bbs = [self.s5_bmm(d * 2 + gq, gq, ub[32 * gq:32 * gq + 32, :]) for gq in range(2)]
            for gq in range(2):
                i = d * 2 + gq
                self.s5_chunk(pr, i, gq, ub[32 * gq:32 * gq + 32, :], (st[:, i, 0:1], st[:, i, 1:2]), d == 1,
                              yb, slice(0, 256), (d == 0 and gq == 0), gq == 1, 64, bbank=bbs[gq])
            cs = slice(c * 256, (c + 1) * 256)

            def evac(d=d, cs=cs, yb=yb):
                if d == 0:
                    P.copy(YS[0:64, cs], yb[0:64, 0:256])
                else:
                    stg = self.stgbuf()
                    P.tt(stg[0:64, 0:256], yb[0:64, 0:256], YS[0:64, cs], ALU.add)
                    P.dma(self.e2_in[l][192:256, cs], stg[0:64, 0:256])
            self.s5_pending.append(evac)
            yield
    self.s5_flush()


def _s5_post(self, l, ysrc, t0, n):
    P = self.P
    CAT = self.av(A_CAT, [128, 8, T])
    Z = self.SQB
    for j in range(2):
        P.act(Z[:, j, 0:n], ysrc[:, j, :], AF.Gelu_apprx_tanh)
    for m in range(2):
        ps = self.bank()
        for k in range(2):
            P.mm(ps[:, 0:n], self.wglu[:, k, m * 128:(m + 1) * 128], Z[:, k, 0:n], start=(k == 0), stop=(k == 1))
        P.act(self.F1[:, 0:n], ps[:, 0:n], AF.Sigmoid)
        P.tt(CAT[:, 2 + m, t0:t0 + n], Z[:, m, 0:n], self.F1[:, 0:n], ALU.mult)


KB.rr = _rr
KB.s5_prep = _s5_prep
KB.s5_chunk = _s5_chunk
KB.s5_bmm = _s5_bmm
KB.s5_flush = _s5_flush
KB.s5_prompt = _s5_prompt
KB.s5_sample = _s5_sample
KB.s5_post = _s5_post


def _layer_norm(self, l, which, t0, n):
    P = self.P
    xT = self.xT
    F1, F2, F3, F4 = self.F1, self.F2, self.F3, self.F4
    s1, s2 = self.B[6], self.B[7]
    for k in range(8):
        st = self.stgbuf()
        P.copy(st[:, 0:n], xT[:, k, t0:t0 + n], eng=ACT)
        P.mm(s1[:, 0:n], self.ones_b[:], st[:, 0:n], start=(k == 0), stop=(k == 7))
        st2 = self.stgbuf()
        P.act(st2[:, 0:n], xT[:, k, t0:t0 + n], AF.Square)
        P.mm(s2[:, 0:n], self.ones_b[:], st2[:, 0:n], start=(k == 0), stop=(k == 7))
    P.ts(F1[:, 0:n], s1[:, 0:n], 1.0 / D, None, ALU.mult)
    P.tt(F2[:, 0:n], F1[:, 0:n], F1[:, 0:n], ALU.mult)
    P.stt(F3[:, 0:n], s2[:, 0:n], 1.0 / D, F2[:, 0:n], ALU.mult, ALU.subtract)
    P.act(F3[:, 0:n], F3[:, 0:n], AF.Ln, bias=self.eps_ln[:, 0:1], scale=1.0)
    P.act(F4[:, 0:n], F3[:, 0:n], AF.Exp, scale=-0.5)
    P.stt(F2[:, 0:n], F1[:, 0:n], -1.0, F4[:, 0:n], ALU.mult, ALU.mult)
    gi, bi = (0, 1) if which == 1 else (2, 3)
    for k in range(8):
        P.tt(F1[:, 0:n], xT[:, k, t0:t0 + n], F4[:, 0:n], ALU.mult)
        P.tt(F3[:, 0:n], F1[:, 0:n], F2[:, 0:n], ALU.add)
        P.ts(xT[:, k, t0:t0 + n], F3[:, 0:n], self.lnp_s[:, l, gi, k:k + 1], self.lnp_s[:, l, bi, k:k + 1],
             ALU.mult, ALU.add)


def _wout_ln1(self, l, t0, n, r):
    P = self.P
    CAT = self.av(A_CAT, [128, 8, T])
    WO = self.av(A_WOUT, [128, 8, D])
    for m in range(8):
        ps = self.bank()
        for k in range(8):
            P.mm(ps[:, 0:n], WO[:, k, m * 128:(m + 1) * 128], CAT[:, k, t0:t0 + n], start=(k == 0), stop=(k == 7))
        P.stt(self.xT[:, m, t0:t0 + n], ps[:, 0:n], self.MS[l][:, 16 + m, r:r + 1], self.xT[:, m, t0:t0 + n],
              ALU.mult, ALU.add)
    self.layer_norm(l, 1, t0, n)


def _ffn(self, l):
    P = self.P
    hT = self.modulate(l, 2)
    WGU = self.bv(B_WGU, [128, 3, 2, 8, 128])
    WD = self.bv(B_WD, [128, 2, NJ, 128])

    def actT(j):
        off = j * T if j < 10 else A_ACT + 15360 + (j - 10) * T
        off = A_ACT + j * T
        return self.av(off, [128, T])
    for j in range(NJ):
        if j == NJ - 3:
            for m0 in range(2):
                P.dma(WD[:, m0], self.w_down[l][:, m0 * 128:(m0 + 1) * 128].rearrange("(j p) n -> p j n", p=128), eng=POOL)
        buf = j % 3
        P.dma(WGU[:, buf, 0], self.w_gate[l][:, j * 128:(j + 1) * 128].rearrange("(k p) n -> p k n", p=128), eng=POOL)
        P.dma(WGU[:, buf, 1], self.w_up[l][:, j * 128:(j + 1) * 128].rearrange("(k p) n -> p k n", p=128), eng=POOL)
        for blk in range(3):
            t0 = blk * 512
            pg = self.bank()
            pu = self.bank()
            for k in range(8):
                P.mm(pg[:, 0:512], WGU[:, buf, 0, k, :], hT[:, k, t0:t0 + 512], start=(k == 0), stop=(k == 7))
            for k in range(8):
                P.mm(pu[:, 0:512], WGU[:, buf, 1, k, :], hT[:, k, t0:t0 + 512], start=(k == 0), stop=(k == 7))
            F = self.F1 if (j * 3 + blk) % 2 == 0 else self.F2
            P.act(F[:, 0:512], pg[:, 0:512], AF.Silu)
            P.tt(actT(j)[:, t0:t0 + 512], F[:, 0:512], pu[:, 0:512], ALU.mult)
    for m in range(8):
        buf = m % 2
        if m >= 2:
            P.dma(WD[:, buf], self.w_down[l][:, m * 128:(m + 1) * 128].rearrange("(j p) n -> p j n", p=128), eng=POOL)
        for blk in range(3):
            t0 = blk * 512
            r = 0 if blk == 0 else 1
            ps = self.bank()
            for j in range(NJ):
                P.mm(ps[:, 0:512], WD[:, buf, j, :], actT(j)[:, t0:t0 + 512], start=(j == 0), stop=(j == NJ - 1))
            P.stt(self.xT[:, m, t0:t0 + 512], ps[:, 0:512], self.MS[l][:, 40 + m, r:r + 1], self.xT[:, m, t0:t0 + 512],
                  ALU.mult, ALU.add)
    for blk in range(3):
        self.layer_norm(l, 2, blk * 512, 512)


def _output(self):
    P = self.P
    xo = self.av(A_XIN, [128, 2, D], F32)
    for tt in range(12):
        buf = xo[:, tt % 2, :]
        for half in range(2):
            bk = self.B[(tt * 2 + half) % 4]
            for kk in range(4):
                k = half * 4 + kk
                P.transpose(bk[:, kk * 128:(kk + 1) * 128], self.xT[:, k, tt * 128:(tt + 1) * 128], self.ident_f[:])
            P.copy(buf[:, half * 512:(half + 1) * 512], bk[:, :], eng=self.cpeng())
        P.dma(self.y_tok[tt * 128:(tt + 1) * 128, :], buf)


def _exchange(self, src, dst):
    def cc(e):
        return e.collective_compute("AllGather", ALU.bypass, replica_groups=self.groups, ins=[src], outs=[dst])
    return self.P.collective(cc, reads=[src], writes=[dst])


def _s5_alloc(self):
    s = self.P.sbuf
    self.s5sm = s("s5sm", [128, 16, 16], F32)
    self.s5prm = s("s5prm", [128, 3, 16], F32)
    self.s5bt = s("s5bt", [128, 2, 16], F32)
    self.s5t16 = s("s5t16", [128, 16], F32)
    self.s5cd = s("s5cd", [32, 2, 128], F32)
    self.s5st = s("s5st", [128, 8, 2], F32)
    self.blall = s("blall", [128, 8, 2, 128], BF16)
    self.diagd = s("diagd", [128, 128], BF16)
    self.s5d_s = s("s5d_s", [128, NL, 2], F32)
    self.s5ds_s = s("s5ds_s", [64, NL], F32)
    self.iota_f = s("iota_f", [128, 256], F32)
    P = self.P
    io = self.iota_f

    def fio(e):
        return e.iota(io[:], [[1, 256]], base=1, channel_multiplier=0, allow_small_or_imprecise_dtypes=True)
    P.add(POOL, fio, writes=[io[:]])
    P.dma(self.s5d_s[:], self.s5d)
    P.dma(self.s5ds_s[:], self.s5ds)


class StopBuild(Exception):
    pass


def _layer(self, l):
    P = self.P
    def chk(tag):
        if self.stop_after == tag:
            raise StopBuild()
    self.modulate(l, 1)
    ws = self.load_layer_weights(l)
    self.phaseC(l, *ws)
    chk("L_c")
    for sh in range(4):
        self.exchange(self.x1_in[l][sh], self.x1_out[l][sh])
    self.exchange(self.g1_in[l], self.g1_out[l])
    chk("L_x1")
    import itertools
    if l == 0:
        self.mods(0, part=1)
    if l + 1 < NL:
        self.mods(l + 1)
    self.side = itertools.chain(self.s5_prompt(l), self.s5_sample(l))
    self.prompt_attention(l)
    chk("L_pa")
    self.sample_loads(l)
    chk("L_sl")
    self.sample_attention(l)
    chk("L_sa")
    for hf in range(2):
        self.exchange(self.e2_in[l][hf * 128:(hf + 1) * 128, :], self.e2_out[l][hf])
    chk("L_x2")
    CAT = self.av(A_CAT, [128, 8, T])
    YST = self.av(A_YST, [128, 2, NS])
    YP = self.av(A_YP, [128, 2, 512])
    eo = self.e2_out[l]
    e4 = eo.rearrange("h (j w) t -> h j w t", j=4)
    o2 = self.own2[l]
    for hf in range(2):
        self.dyn_dma(o2[:, hf * 128:(hf + 1) * 128, :], lambda r, hf=hf: e4[hf, :, :, ds(r, NS)], reads=[eo],
                     writes=[o2[:, hf * 128:(hf + 1) * 128, :]], kind=1)
    for j in range(4):
        half = slice((j % 2) * 64, (j % 2) * 64 + 64)
        P.dma(CAT[half, j // 2, NPR:T], o2[j, 0:64, :])
        P.dma(CAT[:, 4 + j, NPR:T], o2[j, 64:192, :])
        P.dma(YST[half, j // 2, :], o2[j, 192:256, :])
    WO = self.av(A_WOUT, [128, 8, D])
    P.dma(WO, self.w_out[l].rearrange("(k p) n -> p k n", p=128), eng=POOL)
    self.s5_post(l, YP[:, :, :], 0, 512)
    self.wout_ln1(l, 0, 512, 0)
    for sb in range(2):
        self.s5_post(l, YST[:, :, sb * 512:(sb + 1) * 512], NPR + sb * 512, 512)
        self.wout_ln1(l, NPR + sb * 512, 512, 1)
    self.ffn(l)


def build_all(ncores=8, stop_after=None):
    kb = KB(stop_after=stop_after)
    kb.groups = [[0, 1, 2, 3], [4, 5, 6, 7]] if ncores == 8 else [[0, 1, 2, 3]]
    kb.declare()
    kb.s5_alloc()
    kb.stage0()
    try:
        for l in range(NL):
            kb.layer(l)
        kb.output()
    except StopBuild:
        pass
    return kb


KB.layer_norm = _layer_norm
KB.wout_ln1 = _wout_ln1
KB.ffn = _ffn
KB.output = _output
KB.exchange = _exchange
KB.s5_alloc = _s5_alloc
KB.layer = _layer


def rope_tables(r):
    t = np.arange(1024) + r * 1024
    row = (t // 64).astype(np.float32)
    col = (t % 64).astype(np.float32)
    n_freq = 8
    inv = (np.float32(10000.0) ** (-np.arange(n_freq, dtype=np.float32) / np.float32(n_freq))).astype(np.float32)
    ang = np.concatenate([row[:, None] * inv, col[:, None] * inv], -1).astype(np.float32)
    cos = np.cos(ang).astype(np.float32).T
    sin = np.sin(ang).astype(np.float32).T
    c32 = np.concatenate([cos, cos], 0)
    s32 = np.concatenate([sin, sin], 0)
    return np.stack([np.tile(c32, (4, 1)), np.tile(s32, (4, 1))], 0).astype(np.float32)

def tile_rows(a):
    sh = a.shape[:-1]
    return np.ascontiguousarray(np.moveaxis(a.reshape(sh + (a.shape[-1] // 128, 128)), -1, 0))

def make_in_maps(inp):
    f = lambda a: np.ascontiguousarray(a, dtype=np.float32)
    shared = {}
    shared["w_ada"] = f(inp["w_ada"])
    b = inp["b_ada"].reshape(NL, 48, 128)
    shared["badaT"] = f(np.transpose(b, (2, 0, 1)))
    for k in ["w_in", "w_out", "w_gate", "w_up", "w_down", "w_glu", "w_uq", "w_uk", "w_uv"]:
        src = {"w_gate": "ffn_w_gate", "w_up": "ffn_w_up", "w_down": "ffn_w_down", "w_glu": "s5_w_glu",
               "w_uq": "mla_w_uq", "w_uk": "mla_w_uk", "w_uv": "mla_w_uv"}.get(k, k)
        shared[k] = f(inp[src])
    lqk = np.concatenate([inp["diff_lq1"], inp["diff_lk1"], inp["diff_lq2"], inp["diff_lk2"]], -1)
    shared["lqk"] = f(np.broadcast_to(lqk[None], (128, NL, 128)))
    shared["dng"] = f(inp["diff_norm_g"].T)
    shared["gq"] = f(np.transpose(inp["mla_q_norm_g"].reshape(NL, 2, 128), (2, 0, 1)))
    shared["gkv"] = f(inp["mla_kv_norm_g"].T)
    shared["gkv_b"] = f(np.broadcast_to(inp["mla_kv_norm_g"][:, None, :], (NL, 128, 128)))
    ln = np.stack([inp["ln1_g"], inp["ln1_b"], inp["ln2_g"], inp["ln2_b"]], 1)
    shared["lnp"] = f(np.transpose(ln.reshape(NL, 4, 8, 128), (3, 0, 1, 2)))
    logdt = np.broadcast_to(inp["s5_log_dt"][..., None], inp["s5_lam_re"].shape)
    prm = np.stack([inp["s5_lam_re"], inp["s5_lam_im"], logdt], 0)
    prm = prm.reshape(3, NL, 2, 8, 2, 64)
    shared["s5p"] = f(np.transpose(prm, (1, 4, 5, 0, 2, 3)).reshape(NL, 128, 3, 16))
    bb = np.stack([inp["s5_b_re"], inp["s5_b_im"]], 0)
    bb = bb.reshape(2, NL, 2, 8, 2, 64, 16)
    shared["s5b"] = f(np.transpose(bb, (1, 2, 3, 0, 4, 5, 6)).reshape(NL, 16, 2, 128, 16))
    cc = np.stack([inp["s5_c_re"], inp["s5_c_im"]], 0)
    cc = cc.reshape(2, NL, 2, 8, 2, 16, 64)
    shared["s5c"] = f(np.transpose(cc, (1, 2, 3, 0, 4, 5, 6)).reshape(NL, 16, 2, 32, 64))
    dd = inp["s5_d"].reshape(NL, 2, 128)
    shared["s5d"] = f(np.transpose(dd, (2, 0, 1)))
    shared["identd"] = np.eye(128, dtype=np.float32)
    maps = []
    for c in range(8):
        r, b_ = c % 4, c // 4
        m = dict(shared)
        xp = inp["x_prompt"][2 * c:2 * c + 2].reshape(512, 1024)
        xs = inp["x_sample"][b_, r * 1024:(r + 1) * 1024]
        m["x_tok"] = f(np.concatenate([xp, xs], 0))
        cond = np.stack([inp["c_ctx"], inp["c"][b_]], 0)
        m["condT"] = f(np.transpose(cond.reshape(2, 8, 128), (2, 1, 0)))
        m["w_uk_own"] = f(inp["mla_w_uk"][:, :, r * 128:(r + 1) * 128])
        m["w_uv_own"] = f(inp["mla_w_uv"][:, :, r * 128:(r + 1) * 128])
        prs = prm[:, :, :, 2 * r:2 * r + 2]
        m["s5ps"] = f(np.transpose(prs, (1, 4, 5, 0, 2, 3)).reshape(NL, 128, 3, 4))
        bs = bb[:, :, :, 2 * r:2 * r + 2]
        m["s5bs"] = f(np.transpose(bs, (1, 2, 3, 0, 4, 5, 6)).reshape(NL, 4, 2, 128, 16))
        cs = cc[:, :, :, 2 * r:2 * r + 2]
        m["s5cs"] = f(np.transpose(cs, (1, 2, 3, 0, 4, 5, 6)).reshape(NL, 4, 2, 32, 64))
        m["s5ds"] = f(inp["s5_d"][:, 4 * r:4 * r + 4].reshape(NL, 64).T)
        h0 = inp["state_s5"][b_][:, :, 4 * r:4 * r + 4]
        m["s5h0"] = f(h0.reshape(NL, 2, 2, 128, 2).reshape(NL, 4, 128, 2))
        m["c_dk"] = f(inp["cache_diff_k"][b_][:, :, r, :])
        m["c_dv"] = f(inp["cache_diff_v"][b_][:, :, r, :])
        m["c_ckv"] = f(inp["cache_mla_ckv"][b_])
        m["c_kr"] = f(inp["cache_mla_krope"][b_])
        m["rope"] = rope_tables(r)
        maps.append(m)
    return maps


_CACHE = {}


def kernel(**inputs):
    inp = {k: np.asarray(v) for k, v in inputs.items()}
    if "kb" not in _CACHE:
        kb = build_all(8)
        kb.P.emit()
        _CACHE["kb"] = kb
    kb = _CACHE["kb"]
    maps = make_in_maps(inp)
    used = set(kb.ins.keys())
    maps = [{k: v for k, v in m.items() if k in used} for m in maps]
    res = run_bass_kernel_spmd(kb.nc, maps, core_ids=list(range(8)))
    R = res.results
    y_prompt = np.zeros((16, 256, 1024), np.float32)
    y_sample = np.zeros((2, 4096, 1024), np.float32)
    ndk = np.zeros((16, 2, 256, 4, 64), np.float32)
    ndv = np.zeros((16, 2, 256, 4, 64), np.float32)
    nckv = np.zeros((16, 2, 256, 128), np.float32)
    nkr = np.zeros((16, 2, 256, 32), np.float32)
    ns5 = np.zeros((16, 2, 2, 16, 64, 2), np.float32)
    for c in range(8):
        r, b = c % 4, c // 4
        o = R[c]
        y_prompt[2 * c:2 * c + 2] = np.asarray(o["y_tok"][:512]).reshape(2, 256, 1024)
        y_sample[b, r * 1024:(r + 1) * 1024] = np.asarray(o["y_tok"][512:])
        ndk[2 * c:2 * c + 2] = np.asarray(o["ndk"]).reshape(2, 2, 256, 4, 64)
        ndv[2 * c:2 * c + 2] = np.asarray(o["ndv"]).reshape(2, 2, 256, 4, 64)
        nckv[2 * c:2 * c + 2] = np.asarray(o["nckv"])
        nkr[2 * c:2 * c + 2] = np.asarray(o["nkr"])
        ns5[2 * c:2 * c + 2] = np.asarray(o["ns5"])
    return (y_prompt, y_sample, ndk, ndv, nckv, nkr, ns5)
```
